# Optimizing a Trainium2 kernel written in Bass

```python
import jax, jax.numpy as jnp
from jax import lax
import numpy as np

D_MODEL = 1024
BATCH = 2
SEQ = 8192
DEPTH = 4
DEC_BATCH = 32
DEC_SEQ = 1
PAST_LEN = 8192
PAGE_SIZE = 128

HEAD_DIM = 64
N_ATT_HEADS = D_MODEL // (2 * HEAD_DIM)
N_RWKV_HEADS = D_MODEL // HEAD_DIM - N_ATT_HEADS
ATT_W = N_ATT_HEADS * HEAD_DIM
RWKV_W = N_RWKV_HEADS * HEAD_DIM
LORA_DECAY = 64
LORA_ICLR = 64
LORA_GATE = 128
LORA_VRES = 32
ATT_COLS = 3 * ATT_W
RWKV_COLS = 3 * RWKV_W + LORA_DECAY + LORA_ICLR + LORA_GATE
DILATED_PATTERNS = ((128, 1), (512, 4), (2048, 16))
WINDOW_MAX = max(w for w, _ in DILATED_PATTERNS)
BAND_BLOCK = 128
ROPE_THETA = 10000.0
D_FF = 4 * D_MODEL
DEEPNORM_ALPHA = (2.0 * DEPTH) ** 0.25
DEEPNORM_BETA = (8.0 * DEPTH) ** -0.25
LN_EPS = 1e-5
GN_EPS = 64e-5
RMS_EPS = 1e-6
F32 = jnp.float32

kernel_name = "hymba_rwkv7_dilated_swa_deepnorm_step"


def layer_norm(x, g, b):
    xf = x.astype(F32)
    mu = xf.mean(-1, keepdims=True)
    var = jnp.square(xf - mu).mean(-1, keepdims=True)
    return ((xf - mu) * lax.rsqrt(var + LN_EPS) * g.astype(F32) + b.astype(F32)).astype(x.dtype)


def rope(x, pos):
    half = HEAD_DIM // 2
    inv = ROPE_THETA ** (-jnp.arange(half, dtype=F32) * (2.0 / HEAD_DIM))
    ang = pos.astype(F32)[:, None] * inv[None, :]
    cos = jnp.cos(ang)[None, :, None, :]
    sin = jnp.sin(ang)[None, :, None, :]
    xf = x.astype(F32)
    x1, x2 = xf[..., :half], xf[..., half:]
    return jnp.concatenate([x1 * cos - x2 * sin, x1 * sin + x2 * cos], -1).astype(x.dtype)


def combine_dilations(outs, lses):
    wts = jax.nn.softmax(jnp.stack(lses, 0), axis=0)
    return jnp.einsum('pbth,pbthd->bthd', wts, jnp.stack(outs, 0))


def dilated_band_attention(q, k, v, dilation, n_back):
    B, T, H, Dh = q.shape
    span = dilation * BAND_BLOCK
    Tp = -(-T // span) * span
    M = Tp // dilation
    nb = M // BAND_BLOCK

    def to_blocks(a):
        a = jnp.pad(a, ((0, 0), (0, Tp - T), (0, 0), (0, 0)))
        a = a.reshape(B, M, dilation, H, Dh).transpose(0, 2, 1, 3, 4)
        return a.reshape(B, dilation, nb, BAND_BLOCK, H, Dh)

    def with_prev(a):
        prev = jnp.pad(a[:, :, :-1], ((0, 0), (0, 0), (1, 0), (0, 0), (0, 0), (0, 0)))
        return jnp.concatenate([prev, a], axis=3)

    qb = to_blocks(q)
    kb = with_prev(to_blocks(k))
    vb = with_prev(to_blocks(v))
    s = jnp.einsum('brnqhd,brnkhd->brnhqk', qb, kb, preferred_element_type=F32) * (HEAD_DIM ** -0.5)
    qi = jnp.arange(BAND_BLOCK)
    ki = jnp.arange(2 * BAND_BLOCK) - BAND_BLOCK
    dist = qi[:, None] - ki[None, :]
    key_m = jnp.arange(nb)[:, None] * BAND_BLOCK + ki[None, :]
    valid = ((dist >= 0) & (dist <= n_back))[None] & (key_m >= 0)[:, None, :]
    s = jnp.where(valid[None, None, :, None], s, -jnp.inf)
    mx = s.max(-1, keepdims=True)
    e = jnp.exp(s - mx)
    den = e.sum(-1, keepdims=True)
    o = jnp.einsum('brnhqk,brnkhd->brnhqd', e, vb.astype(F32)) / den
    lse = (mx + jnp.log(den))[..., 0]
    o = o.transpose(0, 1, 2, 4, 3, 5).reshape(B, dilation, M, H, Dh)
    o = o.transpose(0, 2, 1, 3, 4).reshape(B, Tp, H, Dh)[:, :T]
    lse = lse.transpose(0, 1, 2, 4, 3).reshape(B, dilation, M, H)
    lse = lse.transpose(0, 2, 1, 3).reshape(B, Tp, H)[:, :T]
    return o, lse


def dilated_attention_prompt(q, k, v):
    outs, lses = [], []
    for window, dilation in DILATED_PATTERNS:
        o, l = dilated_band_attention(q, k, v, dilation, window // dilation)
        outs.append(o)
        lses.append(l)
    return combine_dilations(outs, lses)


def dilated_attention_decode(q, k_new, v_new, k_buf, v_buf):
    L = k_buf.shape[1]
    S = q.shape[1]
    k_ext = jnp.concatenate([k_buf.astype(k_new.dtype), k_new], axis=1)
    v_ext = jnp.concatenate([v_buf.astype(v_new.dtype), v_new], axis=1)
    outs, lses = [], []
    for window, dilation in DILATED_PATTERNS:
        offs = jnp.arange(window // dilation + 1) * dilation
        idx = L + jnp.arange(S)[:, None] - offs[None, :]
        valid = idx >= 0
        idx = jnp.maximum(idx, 0)
        kg = jnp.take(k_ext, idx, axis=1)
        vg = jnp.take(v_ext, idx, axis=1)
        s = jnp.einsum('bshd,bsjhd->bshj', q, kg, preferred_element_type=F32) * (HEAD_DIM ** -0.5)
        s = jnp.where(valid[None, :, None, :], s, -jnp.inf)
        mx = s.max(-1, keepdims=True)
        e = jnp.exp(s - mx)
        den = e.sum(-1, keepdims=True)
        outs.append(jnp.einsum('bshj,bsjhd->bshd', e, vg.astype(F32)) / den)
        lses.append((mx + jnp.log(den))[..., 0])
    return combine_dilations(outs, lses)


def token_shift(p, p_prev0, mu):
    prev = jnp.concatenate([p_prev0[:, None], p[:, :-1]], axis=1)
    return p + (prev - p) * mu


def rwkv7_time_mix(z, v_first, wkv0, lp):
    B, T, _ = z.shape
    H, N = N_RWKV_HEADS, HEAD_DIM
    z = z.astype(F32)
    r = z[..., :RWKV_W]
    k = z[..., RWKV_W:2 * RWKV_W]
    v = z[..., 2 * RWKV_W:3 * RWKV_W]
    o = 3 * RWKV_W
    w_lo = z[..., o:o + LORA_DECAY]
    o += LORA_DECAY
    a_lo = z[..., o:o + LORA_ICLR]
    o += LORA_ICLR
    g_lo = z[..., o:o + LORA_GATE]
    w = -jax.nn.softplus(-(lp['decay_base'].astype(F32) + jnp.tanh(w_lo) @ lp['decay_up'].astype(F32))) - 0.5
    decay = jnp.exp(-jnp.exp(w))
    a = jax.nn.sigmoid(lp['iclr_base'].astype(F32) + a_lo @ lp['iclr_up'].astype(F32))
    g = jax.nn.sigmoid(g_lo) @ lp['gate_up'].astype(F32)
    if v_first is None:
        v_first = v
    else:
        vres_lo = z[..., RWKV_COLS:]
        v = v + (v_first - v) * jax.nn.sigmoid(lp['vres_base'].astype(F32) + vres_lo @ lp['vres_up'].astype(F32))
    kk = (k * lp['key_scale_k'].astype(F32)).reshape(B, T, H, N)
    kk = kk / jnp.maximum(jnp.sqrt(jnp.sum(kk * kk, -1, keepdims=True)), 1e-12)
    k = k * (1.0 + (a - 1.0) * lp['key_scale_a'].astype(F32))
    rh, kh, vh = (t.reshape(B, T, H, N) for t in (r, k, v))
    dh, ah = decay.reshape(B, T, H, N), a.reshape(B, T, H, N)

    def step(S, inp):
        r_t, w_t, k_t, v_t, a_t, b_t = inp
        sa = jnp.einsum('bhvk,bhk->bhv', S, a_t)
        S = S * w_t[:, :, None, :] + sa[..., None] * b_t[:, :, None, :] + v_t[..., None] * k_t[:, :, None, :]
        return S, jnp.einsum('bhvk,bhk->bhv', S, r_t)

    xs = tuple(jnp.moveaxis(t, 1, 0) for t in (rh, dh, kh, vh, -kk, kk * ah))
    S_T, y = lax.scan(step, wkv0.astype(F32), xs)
    y = jnp.moveaxis(y, 0, 1)
    mu = y.mean(-1, keepdims=True)
    var = jnp.square(y - mu).mean(-1, keepdims=True)
    y = ((y - mu) * lax.rsqrt(var + GN_EPS)).reshape(B, T, RWKV_W)
    y = y * lp['gn_g'].astype(F32) + lp['gn_b'].astype(F32)
    bonus = jnp.sum(rh * kh * lp['bonus_rk'].astype(F32).reshape(H, N), -1, keepdims=True) * vh
    y = (y + bonus.reshape(B, T, RWKV_W)) * g
    return y, v_first, S_T


def hybrid_layer(x, pos, x_prev, wkv0, kv_buf, v_first, lp):
    B, T, _ = x.shape
    w_comb = lp['w_in']
    p = x @ w_comb
    q = p[..., :ATT_W].reshape(B, T, N_ATT_HEADS, HEAD_DIM)
    k = p[..., ATT_W:2 * ATT_W].reshape(B, T, N_ATT_HEADS, HEAD_DIM)
    v = p[..., 2 * ATT_W:ATT_COLS].reshape(B, T, N_ATT_HEADS, HEAD_DIM)
    q, k = rope(q, pos), rope(k, pos)
    if kv_buf is None:
        att = dilated_attention_prompt(q, k, v)
    else:
        att = dilated_attention_decode(q, k, v, kv_buf[0], kv_buf[1])
    att = att.reshape(B, T, ATT_W)
    att = att * lax.rsqrt(jnp.mean(jnp.square(att), -1, keepdims=True) + RMS_EPS) * lp['att_gain'].astype(F32)
    pr = p[..., ATT_COLS:]
    if x_prev is None:
        prev0 = jnp.zeros((B, pr.shape[-1]), pr.dtype)
    else:
        prev0 = x_prev.astype(x.dtype) @ w_comb[:, ATT_COLS:]
    z = token_shift(pr, prev0, lp['shift_mu'])
    rw, v_first, S_T = rwkv7_time_mix(z, v_first, wkv0, lp)
    mix = jnp.concatenate([att.astype(x.dtype), rw.astype(x.dtype)], -1) @ lp['w_out']
    x1 = layer_norm(DEEPNORM_ALPHA * x + mix, lp['ln1_g'], lp['ln1_b'])
    h = jnp.square(jax.nn.relu(x1 @ lp['w_ff_up'])) @ lp['w_ff_down']
    y = layer_norm(DEEPNORM_ALPHA * x1 + h, lp['ln2_g'], lp['ln2_b'])
    return y, x[:, -1], S_T, k, v, v_first


def setup_inputs(seed: int = 0) -> dict:
    key = jax.random.key(seed)
    ks = iter(jax.random.split(key, 40))

    def nrm(shape, scale):
        return jax.random.normal(next(ks), shape, F32) * scale

    def unif(shape, lo, hi):
        return jax.random.uniform(next(ks), shape, F32, lo, hi)

    kv_len = min(WINDOW_MAX, PAST_LEN)
    p_main = ATT_COLS + RWKV_COLS
    return {
        "x_prompt": nrm((BATCH, SEQ, D_MODEL), 1.0),
        "x_sample": nrm((DEC_BATCH, DEC_SEQ, D_MODEL), 1.0),
        "state_shift": nrm((DEPTH, DEC_BATCH, D_MODEL), 1.0),
        "state_wkv": nrm((DEPTH, DEC_BATCH, N_RWKV_HEADS, HEAD_DIM, HEAD_DIM), 0.5),
        "cache_k": nrm((DEPTH, DEC_BATCH, kv_len, N_ATT_HEADS, HEAD_DIM), 1.0),
        "cache_v": nrm((DEPTH, DEC_BATCH, kv_len, N_ATT_HEADS, HEAD_DIM), 1.0),
        "w_in": nrm((DEPTH, D_MODEL, p_main), D_MODEL ** -0.5),
        "w_vres_in": nrm((DEPTH - 1, D_MODEL, LORA_VRES), D_MODEL ** -0.5),
        "shift_mu": unif((DEPTH, RWKV_COLS), 0.1, 0.9),
        "vres_mu": unif((DEPTH - 1, LORA_VRES), 0.1, 0.9),
        "decay_base": unif((DEPTH, RWKV_W), -6.0, 0.0),
        "decay_up": nrm((DEPTH, LORA_DECAY, RWKV_W), 0.05),
        "iclr_base": nrm((DEPTH, RWKV_W), 0.2),
        "iclr_up": nrm((DEPTH, LORA_ICLR, RWKV_W), 0.5 * LORA_ICLR ** -0.5),
        "gate_up": nrm((DEPTH, LORA_GATE, RWKV_W), LORA_GATE ** -0.5),
        "vres_base": 1.0 + nrm((DEPTH - 1, RWKV_W), 0.1),
        "vres_up": nrm((DEPTH - 1, LORA_VRES, RWKV_W), 0.1),
        "key_scale_k": 0.85 + nrm((DEPTH, RWKV_W), 0.05),
        "key_scale_a": 1.0 + nrm((DEPTH, RWKV_W), 0.05),
        "bonus_rk": nrm((DEPTH, RWKV_W), 0.1),
        "gn_g": 1.0 + nrm((DEPTH, RWKV_W), 0.05),
        "gn_b": nrm((DEPTH, RWKV_W), 0.02),
        "att_gain": 1.0 + nrm((DEPTH, ATT_W), 0.05),
        "w_out": nrm((DEPTH, D_MODEL, D_MODEL), DEEPNORM_BETA * D_MODEL ** -0.5),
        "ln1_g": 1.0 + nrm((DEPTH, D_MODEL), 0.05),
        "ln1_b": nrm((DEPTH, D_MODEL), 0.02),
        "w_ff_up": nrm((DEPTH, D_MODEL, D_FF), D_MODEL ** -0.5),
        "w_ff_down": nrm((DEPTH, D_FF, D_MODEL), DEEPNORM_BETA * D_FF ** -0.5),
        "ln2_g": 1.0 + nrm((DEPTH, D_MODEL), 0.05),
        "ln2_b": nrm((DEPTH, D_MODEL), 0.02),
    }


def reference(x_prompt, x_sample, state_shift, state_wkv, cache_k, cache_v, w_in, w_vres_in, shift_mu,
              vres_mu, decay_base, decay_up, iclr_base, iclr_up, gate_up, vres_base, vres_up, key_scale_k,
              key_scale_a, bonus_rk, gn_g, gn_b, att_gain, w_out, ln1_g, ln1_b, w_ff_up, w_ff_down, ln2_g,
              ln2_b):
    T_p = x_prompt.shape[1]
    pos_prompt = jnp.arange(T_p)
    pos_sample = PAST_LEN + jnp.arange(x_sample.shape[1])
    keep = min(WINDOW_MAX, T_p)
    wkv_zero = jnp.zeros((x_prompt.shape[0], N_RWKV_HEADS, HEAD_DIM, HEAD_DIM), F32)
    hp, hs = x_prompt, x_sample
    vf_p, vf_s = None, None
    shift_p, shift_s, wkv_p, wkv_s, kp_l, vp_l, ks_l, vs_l = [], [], [], [], [], [], [], []
    for l in range(DEPTH):
        lp = dict(w_in=w_in[l], shift_mu=shift_mu[l], decay_base=decay_base[l], decay_up=decay_up[l],
                  iclr_base=iclr_base[l], iclr_up=iclr_up[l], gate_up=gate_up[l],
                  key_scale_k=key_scale_k[l], key_scale_a=key_scale_a[l], bonus_rk=bonus_rk[l],
                  gn_g=gn_g[l], gn_b=gn_b[l], att_gain=att_gain[l], w_out=w_out[l],
                  ln1_g=ln1_g[l], ln1_b=ln1_b[l], w_ff_up=w_ff_up[l], w_ff_down=w_ff_down[l],
                  ln2_g=ln2_g[l], ln2_b=ln2_b[l])
        if l > 0:
            lp['w_in'] = jnp.concatenate([w_in[l], w_vres_in[l - 1]], axis=1)
            lp['shift_mu'] = jnp.concatenate([shift_mu[l], vres_mu[l - 1]], axis=0)
            lp['vres_base'] = vres_base[l - 1]
            lp['vres_up'] = vres_up[l - 1]
        hp, sp, Sp, kp, vp, vf_p = hybrid_layer(hp, pos_prompt, None, wkv_zero, None, vf_p, lp)
        hs, ss, Ss, kss, vss, vf_s = hybrid_layer(hs, pos_sample, state_shift[l], state_wkv[l],
                                                  (cache_k[l], cache_v[l]), vf_s, lp)
        shift_p.append(sp)
        shift_s.append(ss)
        wkv_p.append(Sp)
        wkv_s.append(Ss)
        kp_l.append(kp[:, T_p - keep:])
        vp_l.append(vp[:, T_p - keep:])
        ks_l.append(kss)
        vs_l.append(vss)
    return (hp, hs, jnp.stack(shift_p), jnp.stack(shift_s), jnp.stack(wkv_p), jnp.stack(wkv_s),
            jnp.stack(kp_l), jnp.stack(vp_l), jnp.stack(ks_l), jnp.stack(vs_l))
```

```python
import numpy as np
from contextlib import ExitStack
import concourse.bass as bass
import concourse.mybir as mybir
from concourse.bass_utils import run_bass_kernel_spmd

F32 = mybir.dt.float32
AF = mybir.ActivationFunctionType
ALU = mybir.AluOpType

ENGS = ("pe", "dve", "act", "pool", "sp")
ENAME = {"pe": "tensor", "dve": "vector", "act": "scalar", "pool": "gpsimd", "sp": "sync"}

D = 1024
HD = 64
PAST = 8192
KVL = 2048
ALPHA = (2.0 * 4) ** 0.25
LN_EPS = 1e-5
GN_EPS = 64e-5
RMS_EPS = 1e-6
NCOL = 1056
WDEC = float(np.exp(-0.5))
import os as _os
FAST_MM = _os.environ.get("FAST_MM", "0") == "1"


class Buf:
    __slots__ = ("name", "w", "r", "dsem", "excl")

    def __init__(self, name):
        self.name = name
        self.w = None
        self.r = {}
        self.dsem = None
        self.excl = False


class DSem:
    def __init__(self, sem, key):
        self.sem = sem
        self.key = key
        self.count = 0


class Sched:
    def __init__(self, nc, es):
        self.nc = nc
        self.es = es
        self.esem = {e: es.enter_context(nc.semaphore("se_" + e)) for e in ENGS}
        self.q = {e: [] for e in ENGS}
        self.cnt = {e: 0 for e in ENGS}
        self.waited = {e: {} for e in ENGS}
        self.dsems = []
        self.dead = False

    def new_dsem(self, name):
        if not hasattr(self, "dsem_by_name"):
            self.dsem_by_name = {}
        if name in self.dsem_by_name:
            return self.dsem_by_name[name]
        d = self._new_dsem(name)
        self.dsem_by_name[name] = d
        return d

    def _new_dsem(self, name):
        d = DSem(self.es.enter_context(self.nc.semaphore("sd_" + name)), "d:" + name + str(len(self.dsems)))
        self.dsems.append(d)
        return d

    def op(self, eng, fn, reads=(), writes=(), dsem=None, inc=16):
        if self.dead:
            return None
        evs = []
        for b in reads:
            if b.w is not None:
                evs.append((b.w, True))
            if b.excl:
                for ev in b.r.values():
                    evs.append((ev, False))
        for b in writes:
            if b.w is not None:
                evs.append((b.w, False))
            for ev in b.r.values():
                evs.append((ev, False))
        waits = []
        wd = self.waited[eng]
        for ((sem, val, key), raw) in evs:
            if key == eng and (eng == "pe" or not raw):
                continue
            if wd.get(key, 0) < val:
                wd[key] = val
                waits.append((sem, val))
        if dsem is None:
            self.cnt[eng] += 1
            ev = (self.esem[eng], self.cnt[eng], eng)
            incv = 1
        else:
            dsem.count += inc
            ev = (dsem.sem, dsem.count, dsem.key)
            incv = inc
        self.q[eng].append((waits, fn, ev[0], incv))
        for b in reads:
            b.r[ev[2]] = ev
        for b in writes:
            b.w = ev
            b.r = {}
        return ev

    def barrier(self):
        if self.dead:
            return
        targets = [(self.esem[f], self.cnt[f], f) for f in ENGS if self.cnt[f] > 0]
        targets += [(d.sem, d.count, d.key) for d in self.dsems if d.count > 0]
        for e in ENGS:
            waits = []
            wd = self.waited[e]
            for (sem, val, key) in targets:
                if key == e:
                    continue
                if wd.get(key, 0) < val:
                    wd[key] = val
                    waits.append((sem, val))
            self.cnt[e] += 1
            self.q[e].append((waits, (lambda eh: eh.nop()), self.esem[e], 1))

    def emit(self, block):
        for e in ENGS:
            items = self.q[e]

            def body(engh, items=items):
                for (waits, fn, sem, incv) in items:
                    for (s, v) in waits:
                        engh.wait_ge(s, v)
                    fn(engh).then_inc(sem, incv)

            getattr(block, ENAME[e])(body)


class Vw:
    __slots__ = ("ap", "bufs", "sb")

    def __init__(self, ap, bufs, sb=True):
        self.ap = ap
        self.bufs = bufs
        self.sb = sb

    def __getitem__(self, i):
        return Vw(self.ap[i], self.bufs, self.sb)

    def re(self, pat, **kw):
        return Vw(self.ap.rearrange(pat, **kw), self.bufs, self.sb)


class KB:
    def ck(self, name):
        import os
        if os.environ.get("KSTOP", "") == name:
            self.S.dead = True

    def __init__(self, T, L, dbg=()):
        self.T = T
        self.L = L
        self.TQ = T // 4
        self.dbg_names = dbg
        self.nc = bass.Bass("TRN2", target_bir_lowering=False)
        self.es = ExitStack()
        self.S = Sched(self.nc, self.es)
        self.out_buf = Buf("outputs")
        self.dram = {}

    def din(self, name, shape):
        t = self.nc.dram_tensor(name, list(shape), F32, kind="ExternalInput")
        v = Vw(t.ap(), [], sb=False)
        self.dram[name] = v
        return v

    def dout(self, name, shape):
        t = self.nc.dram_tensor(name, list(shape), F32, kind="ExternalOutput")
        v = Vw(t.ap(), [self.out_buf], sb=False)
        self.dram[name] = v
        return v

    def dscr(self, name, shape):
        t = self.nc.dram_tensor(name, list(shape), F32)
        return Vw(t.ap(), [Buf(name)], sb=False)

    def sbt(self, name, shape):
        t = self.es.enter_context(self.nc.sbuf_tensor("sb_" + name, list(shape), F32))
        return Vw(t[:], [Buf(name)])

    def pst(self, name, shape):
        t = self.es.enter_context(self.nc.psum_tensor(name, list(shape), F32))
        b = Buf(name)
        b.excl = True
        return Vw(t[:], [b])

    def mm(self, out, lhsT, rhs, start=True, stop=True, fast=False):
        la, ra = lhsT.ap, rhs.ap
        if fast and FAST_MM:
            la = la.bitcast(mybir.dt.float32r)
            ra = ra.bitcast(mybir.dt.float32r)
        self.S.op("pe", lambda e: e.matmul(out.ap, lhsT=la, rhs=ra, start=start, stop=stop),
                  reads=lhsT.bufs + rhs.bufs, writes=out.bufs)

    def tr(self, out, in_, n):
        idn = self.ident[0:n, 0:n]
        self.S.op("pe", lambda e: e.transpose(out=out.ap, in_=in_.ap, identity=idn.ap),
                  reads=in_.bufs + idn.bufs, writes=out.bufs)

    def act(self, out, in_, func, bias=None, scale=None, accum=None):
        kw = {}
        rd = list(in_.bufs)
        if bias is not None:
            if isinstance(bias, Vw):
                kw["bias"] = bias.ap
                rd += bias.bufs
            else:
                kw["bias"] = float(bias)
        if scale is not None:
            if isinstance(scale, Vw):
                kw["scale"] = scale.ap
                rd += scale.bufs
            else:
                kw["scale"] = float(scale)
        wr = list(out.bufs)
        if accum is not None:
            kw["accum_out"] = accum.ap
            wr += accum.bufs
        self.S.op("act", lambda e: e.activation(out=out.ap, in_=in_.ap, func=func, **kw), reads=rd, writes=wr)

    def tt(self, out, a, b, op, eng="dve"):
        self.S.op(eng, lambda e: e.tensor_tensor(out=out.ap, in0=a.ap, in1=b.ap, op=op),
                  reads=a.bufs + b.bufs, writes=out.bufs)

    def ts(self, out, a, s1, op0, s2=None, op1=None, eng="dve"):
        rd = list(a.bufs)

        def cv(s):
            if isinstance(s, Vw):
                rd.extend(s.bufs)
                return s.ap
            return None if s is None else float(s)

        a1, a2 = cv(s1), cv(s2)
        kw = {} if op1 is None else {"op1": op1}
        self.S.op(eng, lambda e: e.tensor_scalar(out=out.ap, in0=a.ap, scalar1=a1, scalar2=a2, op0=op0, **kw),
                  reads=rd, writes=out.bufs)

    def stt(self, out, a, sc, b, op0, op1):
        rd = a.bufs + b.bufs
        if isinstance(sc, Vw):
            rd = rd + sc.bufs
            scv = sc.ap
        else:
            scv = float(sc)
        self.S.op("dve", lambda e: e.scalar_tensor_tensor(out=out.ap, in0=a.ap, scalar=scv, in1=b.ap, op0=op0, op1=op1),
                  reads=rd, writes=out.bufs)

    def cp(self, out, a, eng="dve"):
        if eng == "act":
            self.S.op("act", lambda e: e.copy(out=out.ap, in_=a.ap), reads=a.bufs, writes=out.bufs)
        else:
            self.S.op(eng, lambda e: e.tensor_copy(out=out.ap, in_=a.ap), reads=a.bufs, writes=out.bufs)

    def memset(self, v, val, eng="dve"):
        self.S.op(eng, lambda e: e.memset(v.ap, float(val)), writes=v.bufs)

    def recip(self, out, a):
        self.S.op("dve", lambda e: e.reciprocal(out=out.ap, in_=a.ap), reads=a.bufs, writes=out.bufs)

    def scan(self, out, d0, d1, init, op0, op1):
        rd = d0.bufs + d1.bufs
        self.S.op("dve", lambda e: e.tensor_tensor_scan(out=out.ap, data0=d0.ap, data1=d1.ap, initial=float(init), op0=op0, op1=op1),
                  reads=rd, writes=out.bufs)

    def dma(self, out, in_, eng="sp"):
        sbv = out if out.sb else in_
        b = sbv.bufs[0]
        if b.dsem is None:
            b.dsem = self.S.new_dsem(b.name)
        self.S.op(eng, lambda e: e.dma_start(out=out.ap, in_=in_.ap), reads=in_.bufs, writes=out.bufs, dsem=b.dsem)

    def allgather(self, out, in_):
        if not hasattr(self, "ccsem"):
            self.ccsem = self.S.new_dsem("cc")
        self.S.op("pool", lambda e: e.collective_compute("AllGather", ALU.bypass, replica_groups=[[0, 1, 2, 3], [4, 5, 6, 7]],
                                                         ins=[in_.ap], outs=[out.ap]),
                  reads=in_.bufs, writes=out.bufs, dsem=self.ccsem, inc=1)

    def dbg(self, name, v, shape):
        if name in self.dbg_names:
            o = self.dout("dbg_" + name, shape)
            self.dma(o, v)

    def build(self):
        T, L, TQ = self.T, self.L, self.TQ
        NW = TQ + 4
        NG = T + 16
        P1N = 256
        NT1 = T // P1N
        NMAC = T // 2048
        xT = self.din("xT", [D, T])
        xown = self.din("xown", [D, NW])
        xs = self.din("xs", [D, 16])
        sh = self.din("sh", [L, D, 16])
        wkv = self.din("wkv", [L, 16, 128, 64])
        ck = self.din("ck", [L, 16, KVL, 128])
        cv = self.din("cv", [L, 16, KVL, 128])
        win = self.din("win", [L, 128, 8, NCOL])
        mu_d = self.din("mu", [L, 128, 6])
        pv_d = self.din("pvec", [L, 128, 48])
        lup_d = self.din("lup", [L, 128, 4, 128])
        wout = self.din("wout", [L, 8, 128, 8, 128])
        wup = self.din("wup", [L, 32, 128, 8, 128])
        wdn = self.din("wdn", [L, 8, 128, 4, 1024])
        cst = self.din("cst", [128, 1280])
        ropep = self.din("ropep", [2, 128, T])
        ropes = self.din("ropes", [2, 128, 16])
        sel_d = self.din("sel", [128, 4])
        o_y = self.dout("o_y", [D, NW])
        o_shp = self.dout("o_shp", [L, 128, 8])
        o_shs = self.dout("o_shs", [L, 128, 8, 16])
        o_wkvp = self.dout("o_wkvp", [L, 128, 64])
        o_wkvs = self.dout("o_wkvs", [L, 16, 128, 64])
        o_kp = self.dout("o_kp", [L, 128, KVL])
        o_vp = self.dout("o_vp", [L, 128, KVL])
        o_ks = self.dout("o_ks", [L, 128, 16])
        o_vs = self.dout("o_vs", [L, 128, 16])
        C1 = 1024
        C2 = 256
        NC1 = T // C1
        NC2 = TQ // C2
        ag1_in = [self.dscr("ag1_in%d" % i, [256, C1]) for i in range(NC1)]
        ag1_out = [self.dscr("ag1_out%d" % i, [4 * 256, C1]) for i in range(NC1)]
        ag1s_in = self.dscr("ag1s_in", [256, 16])
        ag1s_out = self.dscr("ag1s_out", [4 * 256, 16])
        ag2_in = [self.dscr("ag2_in%d" % i, [D, C2]) for i in range(NC2)]
        ag2_out = [self.dscr("ag2_out%d" % i, [4 * D, C2]) for i in range(NC2)]
        ag2s_in = self.dscr("ag2s_in", [D, 4])
        ag2s_out = self.dscr("ag2s_out", [4 * D, 4])
        vfirst = self.dscr("vfirst", [128, NG])

        cs = self.sbt("cst", [128, 1280])
        self.ident = cs[:, 0:128]
        perm = cs[:, 128:256]
        bones = cs[:, 256:384]
        amask = cs[:, 384:640]
        MsT = cs[:, 640:768]
        MiT = cs[:, 768:896]
        Ms = cs[:, 896:1024]
        ones = cs[:, 1024:1152]
        sel = self.sbt("sel", [128, 4])
        cvec = self.sbt("cvec", [128, 8])
        mu = self.sbt("mu", [128, 6])
        pv = self.sbt("pv", [128, 48])
        om_ksa = self.sbt("omksa", [128, 1])
        lup = self.sbt("lup", [128, 4, 128])
        Hst = self.sbt("Hst", [128, 64])
        shp_t = self.sbt("shp_t", [128, 8])
        PS = [self.pst("ps%d" % i, [128, 512]) for i in range(8)]
        AW = 44800
        arena_t = self.es.enter_context(self.nc.sbuf_tensor("arena", [128, AW], F32))
        st = {"off": 0}

        def areset():
            st["off"] = 0

        def aalloc(name, n):
            o = st["off"]
            assert o + n <= AW, ("arena overflow", name, o + n)
            st["off"] = o + n
            return Vw(arena_t[:, o:o + n], [Buf(name)])

        self.dma(cs, cst)
        self.dma(sel, sel_d)
        self.memset(cvec[:, 0:1], LN_EPS)
        self.memset(cvec[:, 1:2], GN_EPS)
        self.memset(cvec[:, 2:3], RMS_EPS)
        self.memset(cvec[:, 3:4], 0.0)

        psrr = {"i": 0}

        mmpool = {"p": [0, 1, 2, 3, 4, 5]}

        def nps():
            psrr["i"] = (psrr["i"] + 1) % len(mmpool["p"])
            return PS[mmpool["p"][psrr["i"]]]

        ptrr = {"i": 0}

        def npt():
            ptrr["i"] = (ptrr["i"] + 1) % 2
            return PS[6 + ptrr["i"]]

        PV_DB, PV_IB, PV_VB, PV_KSK, PV_KSA, PV_BON, PV_GNG, PV_GNB = range(8)
        PV_AG = 8
        PV_L1G, PV_L1B, PV_L2G, PV_L2B = 12, 20, 28, 36

        def rwkv_tile(AR, NN, zr, zk, vv, ld, aa, zg_sig_g, YT, state_in, state_out):
            nch = NN // 64
            kk = AR("kk", NN)
            tmp = AR("rtmp", NN)
            tmp2 = AR("rtmp2", NN)
            kp = AR("kp", NN)
            bb = AR("bb", NN)
            Lc = AR("Lc", NN)
            gam = AR("gam", NN)
            Rt = AR("Rt", NN)
            At = AR("At", NN)
            Bt = AR("Bt", NN)
            Kt = AR("Kt", NN)
            Bp = AR("Bp", NN)
            Kp = AR("Kp", NN)
            self.ts(kk, zk, pv[:, PV_KSK:PV_KSK + 1], ALU.mult)
            self.act(tmp, kk, AF.Square)
            for c0 in range(0, NN, 512):
                c1 = min(NN, c0 + 512)
                ps = nps()
                self.mm(ps[:, 0:c1 - c0], bones, tmp[:, c0:c1])
                self.act(tmp2[:, c0:c1], ps[:, 0:c1 - c0], AF.Sqrt)
            self.ts(tmp2, tmp2, 1e-12, ALU.max)
            self.recip(tmp2, tmp2)
            self.tt(kk, kk, tmp2, ALU.mult)
            self.ts(tmp, aa, pv[:, PV_KSA:PV_KSA + 1], ALU.mult, om_ksa[:, 0:1], ALU.add)
            self.tt(kp, zk, tmp, ALU.mult, eng="pool")
            self.tt(bb, kk, aa, ALU.mult, eng="pool")
            for c in range(nch):
                sl = slice(c * 64, (c + 1) * 64)
                self.scan(Lc[:, sl], ones[:, 0:64], ld[:, sl], 0.0, ALU.mult, ALU.add)
            self.act(gam, Lc, AF.Exp)
            self.tt(Rt, zr, gam, ALU.mult)
            self.tt(tmp, Lc, ld, ALU.subtract)
            self.act(tmp, tmp, AF.Exp)
            self.stt(At, kk, -1.0, tmp, ALU.mult, ALU.mult)
            self.act(tmp2, Lc, AF.Exp, scale=-1.0)
            self.tt(Bt, bb, tmp2, ALU.mult)
            self.tt(Kt, kp, tmp2, ALU.mult, eng="pool")
            for c in range(nch):
                sl = slice(c * 64, (c + 1) * 64)
                gC = gam[:, c * 64 + 63:c * 64 + 64]
                self.ts(Bp[:, sl], Bt[:, sl], gC, ALU.mult)
                self.ts(Kp[:, sl], Kt[:, sl], gC, ALU.mult, eng="pool")
            self.ck("rk_pre")
            def mkset(sfx):
                d = {}
                for nm in ("A", "B", "K", "R", "V", "Bp", "Kp"):
                    d["bd" + nm] = AR("bd_" + nm + sfx, 128)
                    self.memset(d["bd" + nm], 0.0, eng="pool")
                d["Pm"] = [AR("Pm%d%s" % (i, sfx), 128) for i in range(2)]
                d["Qm"] = [AR("Qm%d%s" % (i, sfx), 128) for i in range(2)]
                d["Zm"] = [AR("Zm%d%s" % (i, sfx), 128) for i in range(2)]
                for nm, n_ in (("AakT", 128), ("ArbT", 64), ("ArkT", 64), ("Vtb", 128), ("Vts", 64), ("Bptb", 128), ("Kptb", 128), ("mt", 128)):
                    d[nm] = AR(nm + sfx, n_)
                d["cur"] = 0
                return d

            sets = [mkset(""), mkset("_b")]
            bdU = AR("bd_U", 128)
            bdH = AR("bd_H", 128)
            self.memset(bdU, 0.0, eng="pool")
            self.memset(bdH, 0.0, eng="pool")
            RHS = AR("RHS", 64)
            Us = AR("Us", 64)
            no_inv = state_in is not None

            def to_bd(dst, src, e0="dve", e1="pool"):
                self.cp(dst[0:64, 0:64], src[0:64, :], eng=e0)
                self.cp(dst[64:128, 64:128], src[64:128, :], eng=e1)

            def prefix(c, d):
                sl = slice(c * 64, (c + 1) * 64)
                to_bd(d["bdA"], At[:, sl])
                to_bd(d["bdB"], Bt[:, sl], "act", "pool")
                to_bd(d["bdK"], Kt[:, sl])
                to_bd(d["bdR"], Rt[:, sl], "act", "pool")
                to_bd(d["bdV"], vv[:, sl])
                to_bd(d["bdBp"], Bp[:, sl], "act", "pool")
                to_bd(d["bdKp"], Kp[:, sl])
                p1 = nps()
                self.mm(p1[:, 0:128], d["bdA"], d["bdB"])
                self.mm(p1[:, 128:256], d["bdB"], d["bdA"])
                self.mm(p1[:, 256:384], d["bdK"], d["bdA"])
                p2 = nps()
                self.mm(p2[:, 0:128], d["bdB"], d["bdR"])
                self.mm(p2[:, 128:256], d["bdK"], d["bdR"])
                pV = npt()
                self.tr(pV[:, 0:128], d["bdV"], 128)
                self.tt(d["Pm"][0], p1[:, 0:128], Ms, ALU.mult)
                self.tt(d["Qm"][0], p1[:, 128:256], MsT, ALU.mult)
                self.tt(d["AakT"], p1[:, 256:384], MsT, ALU.mult)
                self.tt(d["mt"], p2[:, 0:128], MiT, ALU.mult)
                self.tt(d["ArbT"], d["mt"][:, 0:64], d["mt"][:, 64:128], ALU.add, eng="pool")
                self.tt(d["mt"], p2[:, 128:256], MiT, ALU.mult)
                self.tt(d["ArkT"], d["mt"][:, 0:64], d["mt"][:, 64:128], ALU.add, eng="pool")
                self.cp(d["Vtb"], pV[:, 0:128], eng="act")
                self.tt(d["Vts"], d["Vtb"][:, 0:64], d["Vtb"][:, 64:128], ALU.add, eng="pool")
                p3 = npt()
                self.tr(p3[:, 0:128], d["bdBp"], 128)
                self.tr(p3[:, 128:256], d["bdKp"], 128)
                self.cp(d["Bptb"], p3[:, 0:128], eng="act")
                self.cp(d["Kptb"], p3[:, 128:256], eng="act")
                self.tt(d["Zm"][0], d["Qm"][0], self.ident, ALU.add)
                d["cur"] = 0

            def dround(d):
                cur = d["cur"]
                nx = 1 - cur
                Pm, Qm, Zm = d["Pm"], d["Qm"], d["Zm"]
                pq = nps()
                self.mm(pq[:, 0:128], Pm[cur], Qm[cur])
                self.mm(pq[:, 128:256], Qm[cur], Pm[cur])
                self.cp(Qm[nx], pq[:, 0:128], eng="act")
                self.cp(Pm[nx], pq[:, 128:256], eng="act")
                pz = nps()
                self.mm(pz[:, 0:128], Pm[nx], Zm[cur])
                self.tt(Zm[nx], pz[:, 0:128], Zm[cur], ALU.add)
                d["cur"] = nx

            def suffix(c, d):
                sl = slice(c * 64, (c + 1) * 64)
                if state_in is not None:
                    state_in(c)
                to_bd(bdH, Hst)
                Z = d["Zm"][d["cur"]]
                pr = nps()
                self.mm(pr[:, 0:64], d["bdA"], Hst, start=True, stop=False)
                self.mm(pr[:, 0:64], d["AakT"], d["Vts"], start=False, stop=True)
                self.cp(RHS, pr[:, 0:64], eng="act")
                pu = nps()
                self.mm(pu[:, 0:64], Z, RHS)
                self.cp(Us, pu[:, 0:64], eng="act")
                to_bd(bdU, Us)
                py = nps()
                self.mm(py[:, 0:64], bdH, Rt[:, sl], start=True, stop=False)
                self.mm(py[:, 0:64], bdU, d["ArbT"], start=False, stop=False)
                self.mm(py[:, 0:64], d["Vtb"], d["ArkT"], start=False, stop=True)
                self.cp(YT[:, sl], py[:, 0:64], eng="act")
                ph = nps()
                self.mm(ph[:, 0:64], d["Bptb"], Us, start=True, stop=False)
                self.mm(ph[:, 0:64], d["Kptb"], d["Vts"], start=False, stop=True)
                self.stt(Hst, Hst, gam[:, c * 64 + 63:c * 64 + 64], ph[:, 0:64], ALU.mult, ALU.add)
                if state_out is not None:
                    state_out(c)

            for c0 in range(0, nch, 2):
                if _os.environ.get("SKIP", "") == "rwkv":
                    break
                cs_ = [c for c in (c0, c0 + 1) if c < nch]
                for i_, c in enumerate(cs_):
                    prefix(c, sets[i_])
                if not no_inv:
                    for rnd in range(5):
                        for i_, c in enumerate(cs_):
                            dround(sets[i_])
                for i_, c in enumerate(cs_):
                    suffix(c, sets[i_])
            self.ck("rk_chunks")
            for c0 in range(0, NN, 512):
                c1 = min(NN, c0 + 512)
                w = c1 - c0
                s_ = slice(c0, c1)
                pm = nps()
                self.mm(pm[:, 0:w], bones, YT[:, s_])
                self.act(tmp[:, s_], YT[:, s_], AF.Square)
                pq2 = nps()
                self.mm(pq2[:, 0:w], bones, tmp[:, s_])
                self.ts(tmp2[:, s_], pm[:, 0:w], 1.0 / 64, ALU.mult)
                self.tt(YT[:, s_], YT[:, s_], tmp2[:, s_], ALU.subtract)
                self.tt(tmp2[:, s_], tmp2[:, s_], tmp2[:, s_], ALU.mult)
                self.stt(tmp[:, s_], pq2[:, 0:w], 1.0 / 64, tmp2[:, s_], ALU.mult, ALU.subtract)
                self.act(tmp[:, s_], tmp[:, s_], AF.Sqrt, bias=cvec[:, 1:2])
                self.recip(tmp[:, s_], tmp[:, s_])
                self.tt(YT[:, s_], YT[:, s_], tmp[:, s_], ALU.mult)
                self.ts(YT[:, s_], YT[:, s_], pv[:, PV_GNG:PV_GNG + 1], ALU.mult, pv[:, PV_GNB:PV_GNB + 1], ALU.add)
                self.stt(tmp[:, s_], zr[:, s_], pv[:, PV_BON:PV_BON + 1], kp[:, s_], ALU.mult, ALU.mult)
                pb = nps()
                self.mm(pb[:, 0:w], bones, tmp[:, s_])
                self.tt(tmp[:, s_], pb[:, 0:w], vv[:, s_], ALU.mult)
                self.tt(YT[:, s_], YT[:, s_], tmp[:, s_], ALU.add)
                self.tt(YT[:, s_], YT[:, s_], zg_sig_g[:, s_], ALU.mult)

        def rwkv_prep(AR, NN, l, z, vf_cols, first_layer_store):
            tw = AR("tw", NN)
            ldt = AR("ld", NN)
            at = AR("a", NN)
            gt = AR("g", NN)
            self.act(tw, z["wa"], AF.Tanh)
            self.act(gt, z["g"], AF.Sigmoid)
            for c0 in range(0, NN, 512):
                c1 = min(NN, c0 + 512)
                w = c1 - c0
                s_ = slice(c0, c1)
                p = nps()
                self.mm(p[:, 0:w], lup[:, 0, :], tw[:, s_])
                self.act(ldt[:, s_], p[:, 0:w], AF.Sigmoid, bias=pv[:, PV_DB:PV_DB + 1])
                p = nps()
                self.mm(p[:, 0:w], lup[:, 1, :], z["wa"][:, s_])
                self.act(at[:, s_], p[:, 0:w], AF.Sigmoid, bias=pv[:, PV_IB:PV_IB + 1])
                p = nps()
                self.mm(p[:, 0:w], lup[:, 2, :], gt[:, s_])
                self.cp(gt[:, s_], p[:, 0:w], eng="act")
            self.ts(ldt, ldt, -WDEC, ALU.mult)
            vt = z["v"]
            if l == 0:
                self.dma(vfirst[:, vf_cols], vt)
            else:
                vf = AR("vf", NN)
                vm = AR("vm", NN)
                self.dma(vf, vfirst[:, vf_cols])
                for c0 in range(0, NN, 512):
                    c1 = min(NN, c0 + 512)
                    w = c1 - c0
                    s_ = slice(c0, c1)
                    p = nps()
                    self.mm(p[:, 0:w], lup[0:32, 3, :], z["vr"][0:32, s_])
                    self.act(vm[:, s_], p[:, 0:w], AF.Sigmoid, bias=pv[:, PV_VB:PV_VB + 1])
                self.tt(vf, vf, vt, ALU.subtract)
                self.tt(vf, vf, vm, ALU.mult)
                self.tt(vt, vt, vf, ALU.add)
            return vt, ldt, at, gt

        for l in range(L):
            self.S.barrier()
            areset()

            arc = {}

            def AR(name, n, arc=arc):
                k_ = (name, n)
                if k_ not in arc:
                    arc[k_] = aalloc(name, n)
                return arc[k_]

            Wc = aalloc("Wc", 8 * NCOL).re("p (k n) -> p k n", k=8)
            self.dma(Wc, win[l])
            self.dma(mu, mu_d[l])
            self.dma(pv, pv_d[l])
            self.dma(lup, lup_d[l])
            self.ts(om_ksa, pv[:, PV_KSA:PV_KSA + 1], -1.0, ALU.mult, 1.0, ALU.add)
            self.memset(Hst, 0.0)
            xt = aalloc("xt", 8 * P1N).re("p (k n) -> p k n", k=8)
            xts = aalloc("xts", 8 * 32).re("p (k n) -> p k n", k=8)
            pr6 = aalloc("pr6", 6 * (P1N + 1)).re("p (c n) -> p c n", c=6)
            z6 = aalloc("z6", 6 * P1N).re("p (c n) -> p c n", c=6)
            pq_t = aalloc("pq", P1N)
            pk_t = aalloc("pk", P1N)
            cs_t = aalloc("cossin", 2 * P1N).re("p (c n) -> p c n", c=2)
            KT = [aalloc("KT%d" % i, 2048) for i in range(2)]
            VT = [aalloc("VT%d" % i, 2048) for i in range(2)]
            QT = aalloc("QT", 2048)
            YT = aalloc("YT", P1N)
            acc = aalloc("acc", 2048)
            rb = aalloc("rb", 2048)
            ET = [aalloc("ET%d" % i, 256) for i in range(2)]
            Vg = [aalloc("Vg%d" % i, 130).re("p (h d) -> p h d", h=2) for i in range(2)]
            for i in range(2):
                self.memset(Vg[i], 1.0)
            mark = st["off"]
            self.memset(pr6[:, :, 0:1], 0.0)
            self.ck("init")

            for n in range(NT1):
                if l == 0:
                    src = xT.re("(k p) t -> p k t", p=128)[:, :, n * P1N:(n + 1) * P1N]
                else:
                    r_ = (n * P1N) // TQ
                    loc = (n * P1N) % TQ
                    src = ag2_out[loc // C2].re("(r k p) t -> r p k t", r=4, p=128)[r_]
                self.dma(xt, src)
                if n == NT1 - 1:
                    self.cp(shp_t, xt[:, :, P1N - 1])
                    self.dma(o_shp[l], shp_t)
                self.dma(cs_t, ropep.re("c p t -> p c t")[:, :, n * P1N:(n + 1) * P1N])
                mac = (n * P1N) // 2048
                par = mac % 2
                mcol = (n * P1N) % 2048
                for m in range(9):
                    M = 128 if m < 8 else 32
                    ps = nps()
                    for k in range(8):
                        self.mm(ps[0:M, 0:P1N], Wc[:, k, m * 128:m * 128 + M], xt[:, k, :], start=(k == 0), stop=(k == 7), fast=True)
                    if m == 0:
                        self.cp(pq_t, ps[:, 0:P1N], eng="act")
                    elif m == 1:
                        self.cp(pk_t, ps[:, 0:P1N], eng="act")
                    elif m == 2:
                        self.cp(VT[par][:, mcol:mcol + P1N], ps[:, 0:P1N], eng="act")
                    else:
                        self.cp(pr6[0:M, m - 3, 1:P1N + 1], ps[0:M, 0:P1N], eng="act")
                self.ck("proj")
                for (src_t, dst) in ((pq_t, QT[:, mcol:mcol + P1N]), (pk_t, KT[par][:, mcol:mcol + P1N])):
                    ps = nps()
                    self.mm(ps[:, 0:P1N], perm, src_t)
                    self.tt(dst, ps[:, 0:P1N], cs_t[:, 1, :], ALU.mult)
                    self.tt(src_t, src_t, cs_t[:, 0, :], ALU.mult, eng="pool")
                    self.tt(dst, dst, src_t, ALU.add)
                if n * P1N >= T - KVL:
                    oc = n * P1N - (T - KVL)
                    self.dma(o_kp[l][:, oc:oc + P1N], KT[par][:, mcol:mcol + P1N])
                    self.dma(o_vp[l][:, oc:oc + P1N], VT[par][:, mcol:mcol + P1N])
                self.ck("rope")
                for c in range(6):
                    M = 128 if c < 5 else 32
                    self.tt(z6[0:M, c, :], pr6[0:M, c, 0:P1N], pr6[0:M, c, 1:P1N + 1], ALU.subtract)
                    self.stt(z6[0:M, c, :], z6[0:M, c, :], mu[0:M, c:c + 1], pr6[0:M, c, 1:P1N + 1], ALU.mult, ALU.add)
                    self.cp(pr6[0:M, c, 0:1], pr6[0:M, c, P1N:P1N + 1], eng="pool")
                self.ck("shift")
                z = {"r": z6[:, 0, :], "k": z6[:, 1, :], "v": z6[:, 2, :], "wa": z6[:, 3, :], "g": z6[:, 4, :], "vr": z6[:, 5, :]}
                vt, ldt, at, gt = rwkv_prep(AR, P1N, l, z, slice(n * P1N, (n + 1) * P1N), None)
                self.ck("prep")
                rwkv_tile(AR, P1N, z["r"], z["k"], vt, ldt, at, gt, YT, None, None)
                self.ck("rwkv1")
                self.dma(ag1_in[(n * P1N) // C1][128:256, (n * P1N) % C1:(n * P1N) % C1 + P1N], YT)
                if (n + 1) * P1N % 2048 == 0 and _os.environ.get("SKIP", "") != "att":
                    prev_par = 1 - par
                    mmpool["p"] = [4, 5]
                    for h in range(2):
                        hs = slice(64 * h, 64 * h + 64)
                        first_pat = True
                        for (dil, nblk) in ((1, 16), (4, 4), (16, 1)):
                            pacc = PS[0:4]
                            vgi = 0
                            for r in range(dil):
                                for blk in range(nblk):
                                    u = r * nblk + blk
                                    t0 = r + dil * 128 * blk
                                    cols = slice(t0, t0 + dil * 127 + 1, dil)
                                    have_prev = not (mac == 0 and blk == 0)
                                    if blk > 0:
                                        pt0 = r + dil * 128 * (blk - 1)
                                        pcols = slice(pt0, pt0 + dil * 127 + 1, dil)
                                        kprev, vprev = KT[par], VT[par]
                                    else:
                                        pt0 = r + dil * 128 * (nblk - 1)
                                        pcols = slice(pt0, pt0 + dil * 127 + 1, dil)
                                        kprev, vprev = KT[prev_par], VT[prev_par]
                                    pss = nps()
                                    et = ET[u % 2]
                                    if have_prev:
                                        self.mm(pss[:, 0:128], kprev[hs, pcols], QT[hs, cols])
                                    self.mm(pss[:, 128:256], KT[par][hs, cols], QT[hs, cols])
                                    lo = 0 if have_prev else 128
                                    self.act(et[:, lo:256], pss[:, lo:256], AF.Exp, scale=0.125)
                                    self.tt(et[:, lo:256], et[:, lo:256], amask[:, lo:256], ALU.mult, eng="pool")
                                    pvt = npt()
                                    if have_prev:
                                        self.tr(pvt[:, 0:128], vprev[:, pcols], 128)
                                    self.tr(pvt[:, 128:256], VT[par][:, cols], 128)
                                    vga, vgb = Vg[0], Vg[1]
                                    if have_prev:
                                        self.cp(vga[:, :, 0:64], pvt[:, 0:128].re("p (h d) -> p h d", h=2), eng="dve")
                                    self.cp(vgb[:, :, 0:64], pvt[:, 128:256].re("p (h d) -> p h d", h=2), eng="dve")
                                    pa = pacc[u // 4]
                                    oc = (u % 4) * 128
                                    if have_prev:
                                        self.mm(pa[0:65, oc:oc + 128], vga[:, h, :], et[:, 0:128], start=True, stop=False)
                                    self.mm(pa[0:65, oc:oc + 128], vgb[:, h, :], et[:, 128:256], start=(not have_prev), stop=True)
                            for g4 in range(4):
                                pa = pacc[g4]
                                if dil == 1:
                                    dst = acc[0:65, g4 * 512:(g4 + 1) * 512]
                                    srcv = pa[0:65, :]
                                elif dil == 4:
                                    dst = acc[0:65, :].re("p (b i r) -> p r b i", b=4, i=128, r=4)[:, g4]
                                    srcv = pa[0:65, :].re("p (b i) -> p b i", b=4)
                                else:
                                    dst = acc[0:65, :].re("p (i r) -> p r i", r=16)[:, 4 * g4:4 * g4 + 4]
                                    srcv = pa[0:65, :].re("p (r i) -> p r i", r=4)
                                if first_pat:
                                    self.cp(dst, srcv, eng="dve")
                                else:
                                    self.tt(dst, dst, srcv, ALU.add)
                            first_pat = False
                        for g4 in range(4):
                            pb_ = nps()
                            self.mm(pb_[0:64, :], ones[64:65, 0:64], acc[64:65, g4 * 512:(g4 + 1) * 512])
                            self.recip(rb[0:64, g4 * 512:(g4 + 1) * 512], pb_[0:64, :])
                        self.tt(rb[0:64, :], rb[0:64, :], acc[0:64, :], ALU.mult)
                        for cc_ in range(2048 // C1):
                            self.dma(ag1_in[mac * (2048 // C1) + cc_][64 * h:64 * h + 64, :], rb[0:64, cc_ * C1:(cc_ + 1) * C1])
                    mmpool["p"] = [0, 1, 2, 3, 4, 5]
            self.ck("prompt_mix")
            if l == 0:
                self.dma(xts[:, :, 0:16], xs.re("(k p) s -> p k s", p=128))
            else:
                for r_ in range(4):
                    self.dma(xts[:, :, 4 * r_:4 * r_ + 4], ag2s_out.re("(r k p) t -> r p k t", r=4, p=128)[r_])
            self.dma(xts[:, :, 16:32], sh[l].re("(k p) s -> p k s", p=128))
            self.dma(o_shs[l], xts[:, :, 0:16])
            SN = 16
            ps6 = AR("ps6", 6 * 32).re("p (c n) -> p c n", c=6)
            zs6 = AR("zs6", 6 * SN).re("p (c n) -> p c n", c=6)
            qs = AR("qs", SN)
            ks = AR("ks", SN)
            vs = AR("vs", SN)
            qk_t = AR("qkt", 2 * SN).re("p (c n) -> p c n", c=2)
            css = AR("css", 2 * SN).re("p (c n) -> p c n", c=2)
            self.dma(css, ropes.re("c p s -> p c s"))
            for m in range(9):
                M = 128 if m < 8 else 32
                ps = nps()
                for k in range(8):
                    self.mm(ps[0:M, 0:32], Wc[:, k, m * 128:m * 128 + M], xts[:, k, :], start=(k == 0), stop=(k == 7))
                if m < 2:
                    self.cp(qk_t[:, m, :], ps[:, 0:SN], eng="act")
                elif m == 2:
                    self.cp(vs, ps[:, 0:SN], eng="act")
                else:
                    self.cp(ps6[0:M, m - 3, :], ps[0:M, 0:32], eng="act")
            for (i_, dst) in ((0, qs), (1, ks)):
                ps = nps()
                self.mm(ps[:, 0:SN], perm, qk_t[:, i_, :])
                self.tt(dst, ps[:, 0:SN], css[:, 1, :], ALU.mult)
                self.tt(qk_t[:, i_, :], qk_t[:, i_, :], css[:, 0, :], ALU.mult)
                self.tt(dst, dst, qk_t[:, i_, :], ALU.add)
            self.dma(o_ks[l], ks)
            self.dma(o_vs[l], vs)
            for c in range(6):
                M = 128 if c < 5 else 32
                self.tt(zs6[0:M, c, :], ps6[0:M, c, 16:32], ps6[0:M, c, 0:16], ALU.subtract)
                self.stt(zs6[0:M, c, :], zs6[0:M, c, :], mu[0:M, c:c + 1], ps6[0:M, c, 0:16], ALU.mult, ALU.add)
            self.ck("sproj")
            accs = AR("accs", 2 * SN).re("p (h n) -> p h n", h=2)
            Kc = [AR("Kc%d" % i, 128) for i in range(2)]
            KcT = [AR("KcT%d" % i, 128) for i in range(2)]
            Vc = [AR("Vc%d" % i, 130).re("p (h d) -> p h d", h=2) for i in range(2)]
            es_t = [AR("es%d" % i, 2) for i in range(2)]
            for i in range(2):
                self.memset(Vc[i], 1.0)
            pacc_s = [PS[0], PS[1]]
            mmpool["p"] = [2, 3, 4, 5]
            it = 0
            for s in range(SN):
                for pi, dil in enumerate((1, 4, 16)):
                    b2 = it % 2
                    it += 1
                    rows = slice(KVL - 128 * dil, KVL, dil)
                    self.dma(Kc[b2], ck[l, s][rows, :])
                    self.dma(Vc[b2][:, :, 0:64], cv[l, s][rows, :].re("p (h d) -> p h d", h=2), eng="act")
                    pt = npt()
                    self.tr(pt[:, 0:128], Kc[b2], 128)
                    self.cp(KcT[b2], pt[:, 0:128], eng="act")
                    for h in range(2):
                        hs = slice(64 * h, 64 * h + 64)
                        pz = nps()
                        self.mm(pz[:, 0:1], KcT[b2][hs, :], qs[hs, s:s + 1])
                        self.act(es_t[b2][:, h:h + 1], pz[:, 0:1], AF.Exp, scale=0.125)
                    for h in range(2):
                        self.mm(pacc_s[h][0:65, s:s + 1], Vc[b2][:, h, :], es_t[b2][:, h:h + 1], start=(pi == 0), stop=(pi == 2))
            qkp = AR("qkp", SN)
            e3 = AR("e3", SN)
            self.tt(qkp, qs, ks, ALU.mult)
            ps = nps()
            self.mm(ps[:, 0:SN], bones, qkp)
            self.act(e3, ps[:, 0:SN], AF.Exp, scale=0.125)
            self.ts(e3, e3, 3.0, ALU.mult)
            atts = AR("atts", SN)
            dens = AR("dens", SN)
            for h in range(2):
                hs = slice(64 * h, 64 * h + 64)
                self.cp(accs[0:65, h, :], pacc_s[h][0:65, 0:SN])
                pb_ = nps()
                self.mm(pb_[0:64, 0:SN], ones[64:65, 0:64], accs[64:65, h, :])
                self.cp(dens[hs, :], pb_[0:64, 0:SN])
                self.tt(dens[hs, :], dens[hs, :], e3[hs, :], ALU.add)
                self.cp(atts[hs, :], accs[0:64, h, :])
                self.tt(qkp[hs, :], e3[hs, :], vs[hs, :], ALU.mult)
                self.tt(atts[hs, :], atts[hs, :], qkp[hs, :], ALU.add)
            mmpool["p"] = [0, 1, 2, 3, 4, 5]
            self.recip(dens, dens)
            self.tt(atts, atts, dens, ALU.mult)
            self.dma(ag1s_in[0:128, :], atts)
            self.ck("satt")
            zsd = {"r": zs6[:, 0, :], "k": zs6[:, 1, :], "v": zs6[:, 2, :], "wa": zs6[:, 3, :], "g": zs6[:, 4, :], "vr": zs6[:, 5, :]}
            vt, ldt, at, gt = rwkv_prep(AR, SN, l, zsd, slice(T, T + 16), None)
            EN = 256
            ex = {}
            for nm in ("r", "k", "v", "ld", "a", "g"):
                ex[nm] = AR("ex_" + nm, EN)
                self.memset(ex[nm], 0.0, eng="pool")
            YS = AR("YS", EN)
            Sld = AR("Sld", 64)
            Sbd = AR("Sbd", 128)
            Sout = AR("Sout", 128)
            rws = AR("rws", SN)
            self.memset(Sbd, 0.0)

            def s_out_to(dst):
                self.cp(Sbd[0:64, 0:64], Hst[0:64, :])
                self.cp(Sbd[64:128, 64:128], Hst[64:128, :], eng="pool")
                p = npt()
                self.tr(p[:, 0:128], Sbd, 128)
                self.cp(Sout, p[:, 0:128], eng="act")
                self.dma(dst[0:64, :], Sout[0:64, 0:64])
                self.dma(dst[64:128, :], Sout[64:128, 64:128])

            s_out_to(o_wkvp[l])
            srcs = {"r": zsd["r"], "k": zsd["k"], "v": vt, "ld": ldt, "a": at, "g": gt}
            for grp in range(4):
                for nm in ("r", "k", "v", "ld", "a", "g"):
                    self.cp(ex[nm][:, 0:EN:64], srcs[nm][:, 4 * grp:4 * grp + 4], eng=("dve" if nm in ("r", "v", "a") else "pool"))

                def s_in(c, l=l, grp=grp):
                    self.dma(Sld, wkv[l, 4 * grp + c])
                    self.cp(Sbd[0:64, 0:64], Sld[0:64, :])
                    self.cp(Sbd[64:128, 64:128], Sld[64:128, :], eng="pool")
                    p = npt()
                    self.tr(p[:, 0:128], Sbd, 128)
                    self.cp(Sout, p[:, 0:128], eng="act")
                    self.tt(Hst, Sout[:, 0:64], Sout[:, 64:128], ALU.add)

                def s_out(c, l=l, grp=grp):
                    s_out_to(o_wkvs[l, 4 * grp + c])

                rwkv_tile(AR, EN, ex["r"], ex["k"], ex["v"], ex["ld"], ex["a"], ex["g"], YS, s_in, s_out)
                self.cp(rws[:, 4 * grp:4 * grp + 4], YS[:, 0:EN:64])
            self.dma(ag1s_in[128:256, :], rws)

            self.ck("srwkv")
            for i_ in range(NC1):
                self.allgather(ag1_out[i_], ag1_in[i_])
            self.allgather(ag1s_out, ag1s_in)
            self.ck("ag1")
            self.S.barrier()
            areset()
            ST = min(1024, TQ)
            NST = TQ // ST
            STW = ST + 4
            A_ = aalloc("A", 8 * STW).re("p (k n) -> p k n", k=8)
            B_ = aalloc("B", 8 * STW).re("p (k n) -> p k n", k=8)
            hT = [aalloc("hT0", 4 * STW).re("p (k n) -> p k n", k=4)] * 2
            stg = aalloc("stg", 8 * 512).re("p (k n) -> p k n", k=8)
            sq = aalloc("sq", 512)
            rs = aalloc("rs", STW)
            mean = aalloc("mean", 512)
            rstd = aalloc("rstd", 512)
            wo_b = [aalloc("wo0", 8 * 128).re("p (k n) -> p k n", k=8)] * 2
            wu_b = [aalloc("wu%d" % i, 4 * 8 * 128).re("p (f k n) -> p f k n", f=4, k=8) for i in range(2)]
            wd_b = [aalloc("wd%d" % i, 4 * 1024).re("p (f n) -> p f n", f=4) for i in range(2)]
            for sti in range(NST):
                last = (sti == NST - 1)
                W = ST + (4 if last else 0)
                ctiles = [(c0, 512) for c0 in range(0, ST, 512)]
                if last:
                    ctiles.append((ST, 4))
                for (c0, w) in ctiles:
                    for q in range(4):
                        if c0 < ST:
                            gc = q * TQ + sti * ST + c0
                            gsrc = ag1_out[gc // C1].re("(r c p) t -> p (r c) t", r=4, c=2, p=128)[:, :, gc % C1:gc % C1 + w]
                        else:
                            gsrc = ag1s_out.re("(r c p) t -> p (r c) t", r=4, c=2, p=128)[:, :, 4 * q:4 * q + 4]
                        self.dma(stg[:, :, 0:w], gsrc)
                        if q == 0:
                            self.ts(A_[:, :, c0:c0 + w], stg[:, :, 0:w], sel[:, 0:1], ALU.mult)
                        else:
                            self.stt(A_[:, :, c0:c0 + w], stg[:, :, 0:w], sel[:, q:q + 1], A_[:, :, c0:c0 + w], ALU.mult, ALU.add)
                oc0 = sti * ST
                if l == 0:
                    self.dma(B_[:, :, 0:ST], xown.re("(k p) t -> p k t", p=128)[:, :, oc0:oc0 + ST])
                    if last:
                        self.dma(B_[:, :, ST:ST + 4], xown.re("(k p) t -> p k t", p=128)[:, :, TQ:TQ + 4])
                else:
                    for i_ in range(ST // C2):
                        self.dma(B_[:, :, i_ * C2:(i_ + 1) * C2], ag2_in[oc0 // C2 + i_].re("(k p) t -> p k t", p=128))
                    if last:
                        self.dma(B_[:, :, ST:ST + 4], ag2s_in.re("(k p) t -> p k t", p=128))
                for (c0, w) in ctiles:
                    pm = nps()
                    for r in range(4):
                        self.act(sq[:, 0:w], A_[:, 2 * r, c0:c0 + w], AF.Square)
                        self.mm(pm[:, 0:w], ones, sq[:, 0:w], start=(r == 0), stop=(r == 3))
                    self.act(rs[:, c0:c0 + w], pm[:, 0:w], AF.Sqrt, bias=cvec[:, 2:3], scale=1.0 / 512)
                self.recip(rs[:, 0:W], rs[:, 0:W])
                for r in range(4):
                    self.stt(A_[:, 2 * r, 0:W], A_[:, 2 * r, 0:W], pv[:, PV_AG + r:PV_AG + r + 1], rs[:, 0:W], ALU.mult, ALU.mult)
                for m in range(8):
                    wb = wo_b[m % 2]
                    self.dma(wb, wout[l, m])
                    for (c0, w) in ctiles:
                        ps = nps()
                        for kc in range(8):
                            src_k = 2 * kc if kc < 4 else 2 * (kc - 4) + 1
                            self.mm(ps[:, 0:w], wb[:, kc, :], A_[:, src_k, c0:c0 + w], start=(kc == 0), stop=(kc == 7), fast=(w >= 256))
                        self.stt(B_[:, m, c0:c0 + w], B_[:, m, c0:c0 + w], ALPHA, ps[:, 0:w], ALU.mult, ALU.add)

                def layer_norm(X, gcol, bcol):
                    for (c0, w) in ctiles:
                        pm = nps()
                        pq2 = nps()
                        for k in range(8):
                            self.mm(pm[:, 0:w], ones, X[:, k, c0:c0 + w], start=(k == 0), stop=(k == 7))
                        for k in range(8):
                            self.act(sq[:, 0:w], X[:, k, c0:c0 + w], AF.Square)
                            self.mm(pq2[:, 0:w], ones, sq[:, 0:w], start=(k == 0), stop=(k == 7))
                        self.ts(mean[:, 0:w], pm[:, 0:w], 1.0 / D, ALU.mult)
                        self.tt(sq[:, 0:w], mean[:, 0:w], mean[:, 0:w], ALU.mult)
                        self.stt(rstd[:, 0:w], pq2[:, 0:w], 1.0 / D, sq[:, 0:w], ALU.mult, ALU.subtract)
                        self.act(rstd[:, 0:w], rstd[:, 0:w], AF.Sqrt, bias=cvec[:, 0:1])
                        self.recip(rstd[:, 0:w], rstd[:, 0:w])
                        for k in range(8):
                            eng = "dve" if k % 2 == 0 else "pool"
                            self.tt(X[:, k, c0:c0 + w], X[:, k, c0:c0 + w], mean[:, 0:w], ALU.subtract, eng=eng)
                            self.tt(X[:, k, c0:c0 + w], X[:, k, c0:c0 + w], rstd[:, 0:w], ALU.mult, eng=eng)
                            self.ts(X[:, k, c0:c0 + w], X[:, k, c0:c0 + w], pv[:, gcol + k:gcol + k + 1], ALU.mult,
                                    pv[:, bcol + k:bcol + k + 1], ALU.add)

                layer_norm(B_, PV_L1G, PV_L1B)
                for k in range(8):
                    self.ts(A_[:, k, 0:W], B_[:, k, 0:W], ALPHA, ALU.mult, eng=("dve" if k % 2 == 0 else "pool"))
                for g in range(8):
                    hb = hT[g % 2]
                    wub = wu_b[g % 2]
                    wdb = wd_b[g % 2]
                    self.dma(wub, wup[l, 4 * g:4 * g + 4].re("f p k n -> p f k n"))
                    self.dma(wdb, wdn[l, g], eng="act")
                    for f in range(4):
                        for (c0, w) in ctiles:
                            ps = nps()
                            for kc in range(8):
                                self.mm(ps[:, 0:w], wub[:, f, kc, :], B_[:, kc, c0:c0 + w], start=(kc == 0), stop=(kc == 7), fast=(w >= 256))
                            self.act(hb[:, f, c0:c0 + w], ps[:, 0:w], AF.Relu)
                            self.tt(hb[:, f, c0:c0 + w], hb[:, f, c0:c0 + w], hb[:, f, c0:c0 + w], ALU.mult, eng="pool")
                    for m in range(8):
                        for (c0, w) in ctiles:
                            ps = nps()
                            for f in range(4):
                                self.mm(ps[:, 0:w], wdb[:, f, m * 128:(m + 1) * 128], hb[:, f, c0:c0 + w], start=(f == 0), stop=(f == 3), fast=(w >= 256))
                            self.tt(A_[:, m, c0:c0 + w], A_[:, m, c0:c0 + w], ps[:, 0:w], ALU.add)
                layer_norm(A_, PV_L2G, PV_L2B)
                if l == L - 1:
                    self.dma(o_y.re("(k p) t -> p k t", p=128)[:, :, oc0:oc0 + ST], A_[:, :, 0:ST])
                    if last:
                        self.dma(o_y.re("(k p) t -> p k t", p=128)[:, :, TQ:TQ + 4], A_[:, :, ST:ST + 4])
                else:
                    for i_ in range(ST // C2):
                        self.dma(ag2_in[oc0 // C2 + i_].re("(k p) t -> p k t", p=128), A_[:, :, i_ * C2:(i_ + 1) * C2])
                    if last:
                        self.dma(ag2s_in.re("(k p) t -> p k t", p=128), A_[:, :, ST:ST + 4])
            if l < L - 1:
                for i_ in range(NC2):
                    self.allgather(ag2_out[i_], ag2_in[i_])
                self.allgather(ag2s_out, ag2s_in)

        self.S.dead = False
        self.S.barrier()
        block = self.es.enter_context(self.nc.Block())
        self.S.emit(block)
        self.es.close()
        return self.nc


def _consts(T):
    ident = np.eye(128, dtype=np.float32)
    perm = np.zeros((128, 128), np.float32)
    for m in range(128):
        h, i = divmod(m, 64)
        perm[h * 64 + (i + 32) % 64, m] = 1.0
    bones = np.kron(np.eye(2, dtype=np.float32), np.ones((64, 64), np.float32))
    kq = np.arange(128)
    prevm = (kq[:, None] >= kq[None, :]).astype(np.float32)
    curm = (kq[:, None] <= kq[None, :]).astype(np.float32)
    amask = np.concatenate([prevm, curm], 1)
    i64 = np.arange(64)
    sT = (i64[:, None] < i64[None, :]).astype(np.float32)
    iT = (i64[:, None] <= i64[None, :]).astype(np.float32)
    sN = (i64[None, :] < i64[:, None]).astype(np.float32)
    e2 = np.eye(2, dtype=np.float32)
    MsT, MiT, Ms = np.kron(e2, sT), np.kron(e2, iT), np.kron(e2, sN)
    ones = np.ones((128, 128), np.float32)
    pad = np.zeros((128, 128), np.float32)
    cst = np.concatenate([ident, perm, bones, amask, MsT, MiT, Ms, ones, pad], 1).astype(np.float32)
    half = 32
    inv = (10000.0 ** (-np.arange(half, dtype=np.float32) * np.float32(2.0 / 64))).astype(np.float32)

    def tab(pos):
        ang = pos.astype(np.float32)[:, None] * inv[None, :]
        c, s = np.cos(ang).astype(np.float32), np.sin(ang).astype(np.float32)
        cos64 = np.concatenate([c, c], 1).T
        sin64 = np.concatenate([-s, s], 1).T
        return np.stack([np.tile(cos64, (2, 1)), np.tile(sin64, (2, 1))]).astype(np.float32)

    ropep = tab(np.arange(T))
    ropes = tab(np.full((16,), PAST))
    return cst, ropep, ropes


_CACHE = {}


def _get_nc(T, L, dbg=()):
    key = (T, L, tuple(dbg))
    if key not in _CACHE:
        _CACHE[key] = KB(T, L, dbg).build()
    return _CACHE[key]


def run(inp, T, L, dbg=()):
    f = lambda a: np.ascontiguousarray(a, dtype=np.float32)
    TQ = T // 4
    cst, ropep, ropes = _consts(T)
    ATT_W = 512
    RW0 = 3 * ATT_W
    in_maps = []
    for c in range(8):
        b, j = divmod(c, 4)
        hsl = slice(128 * j, 128 * j + 128)
        sg = slice(16 * b, 16 * b + 16)
        m = {}
        m["xT"] = f(inp["x_prompt"][b].T)
        own = np.concatenate([inp["x_prompt"][b, j * TQ:(j + 1) * TQ], inp["x_sample"][16 * b + 4 * j:16 * b + 4 * j + 4, 0]], 0)
        m["xown"] = f(own.T)
        m["xs"] = f(inp["x_sample"][sg, 0].T)
        m["sh"] = f(np.transpose(inp["state_shift"][:L, sg], (0, 2, 1)))
        m["wkv"] = f(inp["state_wkv"][:L, sg, 2 * j:2 * j + 2].reshape(L, 16, 128, 64))
        m["ck"] = f(inp["cache_k"][:L, sg, :, 2 * j:2 * j + 2].reshape(L, 16, KVL, 128))
        m["cv"] = f(inp["cache_v"][:L, sg, :, 2 * j:2 * j + 2].reshape(L, 16, KVL, 128))
        win = np.zeros((L, D, NCOL), np.float32)
        mu = np.zeros((L, 128, 6), np.float32)
        pvec = np.zeros((L, 128, 48), np.float32)
        lup = np.zeros((L, 128, 4, 128), np.float32)
        for l in range(L):
            w = inp["w_in"][l]
            cols = []
            for blk in range(3):
                cols.append(w[:, blk * ATT_W + 128 * j: blk * ATT_W + 128 * j + 128])
            for blk in range(3):
                cols.append(w[:, RW0 + blk * 512 + 128 * j: RW0 + blk * 512 + 128 * j + 128])
            o = RW0 + 3 * 512
            cols.append(w[:, o:o + 128])
            cols.append(w[:, o + 128:o + 256])
            win[l, :, :1024] = np.concatenate(cols, 1)
            smu = inp["shift_mu"][l]
            for blk in range(3):
                mu[l, :, blk] = smu[blk * 512 + 128 * j: blk * 512 + 128 * j + 128]
            mu[l, :, 3] = smu[1536:1664]
            mu[l, :, 4] = smu[1664:1792]
            if l > 0:
                win[l, :, 1024:1056] = inp["w_vres_in"][l - 1]
                mu[l, :32, 5] = inp["vres_mu"][l - 1]
                pvec[l, :, 2] = inp["vres_base"][l - 1][hsl]
                lup[l, :32, 3, :] = inp["vres_up"][l - 1][:, hsl]
            pvec[l, :, 0] = inp["decay_base"][l][hsl]
            pvec[l, :, 1] = inp["iclr_base"][l][hsl]
            pvec[l, :, 3] = inp["key_scale_k"][l][hsl]
            pvec[l, :, 4] = inp["key_scale_a"][l][hsl]
            pvec[l, :, 5] = inp["bonus_rk"][l][hsl]
            pvec[l, :, 6] = inp["gn_g"][l][hsl]
            pvec[l, :, 7] = inp["gn_b"][l][hsl]
            pvec[l, :, 8:12] = inp["att_gain"][l].reshape(4, 128).T
            pvec[l, :, 12:20] = inp["ln1_g"][l].reshape(8, 128).T
            pvec[l, :, 20:28] = inp["ln1_b"][l].reshape(8, 128).T
            pvec[l, :, 28:36] = inp["ln2_g"][l].reshape(8, 128).T
            pvec[l, :, 36:44] = inp["ln2_b"][l].reshape(8, 128).T
            lup[l, :64, 0, :] = inp["decay_up"][l][:, hsl]
            lup[l, 64:, 1, :] = inp["iclr_up"][l][:, hsl]
            lup[l, :, 2, :] = inp["gate_up"][l][:, hsl]
        m["win"] = f(win.reshape(L, 8, 128, NCOL).transpose(0, 2, 1, 3))
        m["mu"] = mu
        m["pvec"] = pvec
        m["lup"] = lup
        m["wout"] = f(inp["w_out"][:L].reshape(L, 8, 128, 8, 128).transpose(0, 3, 2, 1, 4))
        m["wup"] = f(inp["w_ff_up"][:L].reshape(L, 8, 128, 32, 128).transpose(0, 3, 2, 1, 4))
        m["wdn"] = f(inp["w_ff_down"][:L].reshape(L, 8, 4, 128, 1024).transpose(0, 1, 3, 2, 4))
        m["cst"] = cst
        m["ropep"] = ropep
        m["ropes"] = ropes
        sel = np.zeros((128, 4), np.float32)
        sel[:, j] = 1.0
        m["sel"] = sel
        in_maps.append(m)
    nc = _get_nc(T, L, dbg)
    res = run_bass_kernel_spmd(nc, in_maps, core_ids=list(range(8)))
    R = res.results
    B = 2
    y_p = np.zeros((B, T, D), np.float32)
    y_s = np.zeros((32, 1, D), np.float32)
    shp = np.zeros((L, B, D), np.float32)
    shs = np.zeros((L, 32, D), np.float32)
    wkvp = np.zeros((L, B, 8, 64, 64), np.float32)
    wkvs = np.zeros((L, 32, 8, 64, 64), np.float32)
    kp = np.zeros((L, B, KVL, 8, 64), np.float32)
    vp = np.zeros((L, B, KVL, 8, 64), np.float32)
    ksn = np.zeros((L, 32, 1, 8, 64), np.float32)
    vsn = np.zeros((L, 32, 1, 8, 64), np.float32)
    for c in range(8):
        b, j = divmod(c, 4)
        r = R[c]
        oy = r["o_y"]
        y_p[b, j * TQ:(j + 1) * TQ] = oy[:, :TQ].T
        y_s[16 * b + 4 * j:16 * b + 4 * j + 4, 0] = oy[:, TQ:].T
        if j == 0:
            shp[:, b] = r["o_shp"].transpose(0, 2, 1).reshape(L, D)
            shs[:, 16 * b:16 * b + 16] = r["o_shs"].transpose(0, 3, 2, 1).reshape(L, 16, D)
        wkvp[:, b, 2 * j:2 * j + 2] = r["o_wkvp"].reshape(L, 2, 64, 64)
        wkvs[:, 16 * b:16 * b + 16, 2 * j:2 * j + 2] = r["o_wkvs"].reshape(L, 16, 2, 64, 64)
        kp[:, b, :, 2 * j:2 * j + 2] = r["o_kp"].reshape(L, 2, 64, KVL).transpose(0, 3, 1, 2)
        vp[:, b, :, 2 * j:2 * j + 2] = r["o_vp"].reshape(L, 2, 64, KVL).transpose(0, 3, 1, 2)
        ksn[:, 16 * b:16 * b + 16, 0, 2 * j:2 * j + 2] = r["o_ks"].reshape(L, 2, 64, 16).transpose(0, 3, 1, 2)
        vsn[:, 16 * b:16 * b + 16, 0, 2 * j:2 * j + 2] = r["o_vs"].reshape(L, 2, 64, 16).transpose(0, 3, 1, 2)
    outs = (y_p, y_s, shp, shs, wkvp, wkvs, kp, vp, ksn, vsn)
    return outs, R


def kernel(**inputs):
    inp = {k: np.asarray(v) for k, v in inputs.items()}
    outs, _ = run(inp, T=8192, L=4)
    return outs
```

```python
import numpy as np
from contextlib import ExitStack
import concourse.bass as bass
import concourse.mybir as mybir
from concourse.bass_utils import run_bass_kernel_spmd

F32 = mybir.dt.float32
AF = mybir.ActivationFunctionType
ALU = mybir.AluOpType

ENGS = ("pe", "dve", "act", "pool", "sp")
ENAME = {"pe": "tensor", "dve": "vector", "act": "scalar", "pool": "gpsimd", "sp": "sync"}

D = 1024
HD = 64
PAST = 8192
KVL = 2048
ALPHA = (2.0 * 4) ** 0.25
LN_EPS = 1e-5
GN_EPS = 64e-5
RMS_EPS = 1e-6
NCOL = 1056
WDEC = float(np.exp(-0.5))
import os as _os
FAST_MM = _os.environ.get("FAST_MM", "0") == "1"


class Buf:
    __slots__ = ("name", "w", "r", "dsem", "excl")

    def __init__(self, name):
        self.name = name
        self.w = None
        self.r = {}
        self.dsem = None
        self.excl = False


class DSem:
    def __init__(self, sem, key):
        self.sem = sem
        self.key = key
        self.count = 0


class Sched:
    def __init__(self, nc, es):
        self.nc = nc
        self.es = es
        self.esem = {e: es.enter_context(nc.semaphore("se_" + e)) for e in ENGS}
        self.q = {e: [] for e in ENGS}
        self.cnt = {e: 0 for e in ENGS}
        self.waited = {e: {} for e in ENGS}
        self.dsems = []
        self.dead = False

    def new_dsem(self, name):
        if not hasattr(self, "dsem_by_name"):
            self.dsem_by_name = {}
        if name in self.dsem_by_name:
            return self.dsem_by_name[name]
        d = self._new_dsem(name)
        self.dsem_by_name[name] = d
        return d

    def _new_dsem(self, name):
        d = DSem(self.es.enter_context(self.nc.semaphore("sd_" + name)), "d:" + name + str(len(self.dsems)))
        self.dsems.append(d)
        return d

    def op(self, eng, fn, reads=(), writes=(), dsem=None, inc=16):
        if self.dead:
            return None
        evs = []
        for b in reads:
            if b.w is not None:
                evs.append((b.w, True))
            if b.excl:
                for ev in b.r.values():
                    evs.append((ev, False))
        for b in writes:
            if b.w is not None:
                evs.append((b.w, False))
            for ev in b.r.values():
                evs.append((ev, False))
        waits = []
        wd = self.waited[eng]
        for ((sem, val, key), raw) in evs:
            if key == eng and (eng == "pe" or not raw):
                continue
            if wd.get(key, 0) < val:
                wd[key] = val
                waits.append((sem, val))
        if dsem is None:
            self.cnt[eng] += 1
            ev = (self.esem[eng], self.cnt[eng], eng)
            incv = 1
        else:
            dsem.count += inc
            ev = (dsem.sem, dsem.count, dsem.key)
            incv = inc
        self.q[eng].append((waits, fn, ev[0], incv))
        for b in reads:
            b.r[ev[2]] = ev
        for b in writes:
            b.w = ev
            b.r = {}
        return ev

    def barrier(self):
        if self.dead:
            return
        targets = [(self.esem[f], self.cnt[f], f) for f in ENGS if self.cnt[f] > 0]
        targets += [(d.sem, d.count, d.key) for d in self.dsems if d.count > 0]
        for e in ENGS:
            waits = []
            wd = self.waited[e]
            for (sem, val, key) in targets:
                if key == e:
                    continue
                if wd.get(key, 0) < val:
                    wd[key] = val
                    waits.append((sem, val))
            self.cnt[e] += 1
            self.q[e].append((waits, (lambda eh: eh.nop()), self.esem[e], 1))

    def emit(self, block):
        for e in ENGS:
            items = self.q[e]

            def body(engh, items=items):
                for (waits, fn, sem, incv) in items:
                    for (s, v) in waits:
                        engh.wait_ge(s, v)
                    fn(engh).then_inc(sem, incv)

            getattr(block, ENAME[e])(body)


class Vw:
    __slots__ = ("ap", "bufs", "sb")

    def __init__(self, ap, bufs, sb=True):
        self.ap = ap
        self.bufs = bufs
        self.sb = sb

    def __getitem__(self, i):
        return Vw(self.ap[i], self.bufs, self.sb)

    def re(self, pat, **kw):
        return Vw(self.ap.rearrange(pat, **kw), self.bufs, self.sb)


class KB:
    def ck(self, name):
        import os
        if os.environ.get("KSTOP", "") == name:
            self.S.dead = True

    def __init__(self, T, L, dbg=()):
        self.T = T
        self.L = L
        self.TQ = T // 4
        self.dbg_names = dbg
        self.nc = bass.Bass("TRN2", target_bir_lowering=False)
        self.es = ExitStack()
        self.S = Sched(self.nc, self.es)
        self.out_buf = Buf("outputs")
        self.dram = {}

    def din(self, name, shape):
        t = self.nc.dram_tensor(name, list(shape), F32, kind="ExternalInput")
        v = Vw(t.ap(), [], sb=False)
        self.dram[name] = v
        return v

    def dout(self, name, shape):
        t = self.nc.dram_tensor(name, list(shape), F32, kind="ExternalOutput")
        v = Vw(t.ap(), [self.out_buf], sb=False)
        self.dram[name] = v
        return v

    def dscr(self, name, shape):
        t = self.nc.dram_tensor(name, list(shape), F32)
        return Vw(t.ap(), [Buf(name)], sb=False)

    def sbt(self, name, shape):
        t = self.es.enter_context(self.nc.sbuf_tensor("sb_" + name, list(shape), F32))
        return Vw(t[:], [Buf(name)])

    def pst(self, name, shape):
        t = self.es.enter_context(self.nc.psum_tensor(name, list(shape), F32))
        b = Buf(name)
        b.excl = True
        return Vw(t[:], [b])

    def mm(self, out, lhsT, rhs, start=True, stop=True, fast=False):
        la, ra = lhsT.ap, rhs.ap
        if fast and FAST_MM:
            la = la.bitcast(mybir.dt.float32r)
            ra = ra.bitcast(mybir.dt.float32r)
        self.S.op("pe", lambda e: e.matmul(out.ap, lhsT=la, rhs=ra, start=start, stop=stop),
                  reads=lhsT.bufs + rhs.bufs, writes=out.bufs)

    def tr(self, out, in_, n):
        idn = self.ident[0:n, 0:n]
        self.S.op("pe", lambda e: e.transpose(out=out.ap, in_=in_.ap, identity=idn.ap),
                  reads=in_.bufs + idn.bufs, writes=out.bufs)

    def act(self, out, in_, func, bias=None, scale=None, accum=None):
        kw = {}
        rd = list(in_.bufs)
        if bias is not None:
            if isinstance(bias, Vw):
                kw["bias"] = bias.ap
                rd += bias.bufs
            else:
                kw["bias"] = float(bias)
        if scale is not None:
            if isinstance(scale, Vw):
                kw["scale"] = scale.ap
                rd += scale.bufs
            else:
                kw["scale"] = float(scale)
        wr = list(out.bufs)
        if accum is not None:
            kw["accum_out"] = accum.ap
            wr += accum.bufs
        self.S.op("act", lambda e: e.activation(out=out.ap, in_=in_.ap, func=func, **kw), reads=rd, writes=wr)

    def tt(self, out, a, b, op, eng="dve"):
        self.S.op(eng, lambda e: e.tensor_tensor(out=out.ap, in0=a.ap, in1=b.ap, op=op),
                  reads=a.bufs + b.bufs, writes=out.bufs)

    def ts(self, out, a, s1, op0, s2=None, op1=None, eng="dve"):
        rd = list(a.bufs)

        def cv(s):
            if isinstance(s, Vw):
                rd.extend(s.bufs)
                return s.ap
            return None if s is None else float(s)

        a1, a2 = cv(s1), cv(s2)
        kw = {} if op1 is None else {"op1": op1}
        self.S.op(eng, lambda e: e.tensor_scalar(out=out.ap, in0=a.ap, scalar1=a1, scalar2=a2, op0=op0, **kw),
                  reads=rd, writes=out.bufs)

    def stt(self, out, a, sc, b, op0, op1):
        rd = a.bufs + b.bufs
        if isinstance(sc, Vw):
            rd = rd + sc.bufs
            scv = sc.ap
        else:
            scv = float(sc)
        self.S.op("dve", lambda e: e.scalar_tensor_tensor(out=out.ap, in0=a.ap, scalar=scv, in1=b.ap, op0=op0, op1=op1),
                  reads=rd, writes=out.bufs)

    def cp(self, out, a, eng="dve"):
        if eng == "act":
            self.S.op("act", lambda e: e.copy(out=out.ap, in_=a.ap), reads=a.bufs, writes=out.bufs)
        else:
            self.S.op(eng, lambda e: e.tensor_copy(out=out.ap, in_=a.ap), reads=a.bufs, writes=out.bufs)

    def memset(self, v, val, eng="dve"):
        self.S.op(eng, lambda e: e.memset(v.ap, float(val)), writes=v.bufs)

    def recip(self, out, a):
        self.S.op("dve", lambda e: e.reciprocal(out=out.ap, in_=a.ap), reads=a.bufs, writes=out.bufs)

    def scan(self, out, d0, d1, init, op0, op1):
        rd = d0.bufs + d1.bufs
        self.S.op("dve", lambda e: e.tensor_tensor_scan(out=out.ap, data0=d0.ap, data1=d1.ap, initial=float(init), op0=op0, op1=op1),
                  reads=rd, writes=out.bufs)

    def dma(self, out, in_, eng="sp"):
        sbv = out if out.sb else in_
        b = sbv.bufs[0]
        if b.dsem is None:
            b.dsem = self.S.new_dsem(b.name)
        self.S.op(eng, lambda e: e.dma_start(out=out.ap, in_=in_.ap), reads=in_.bufs, writes=out.bufs, dsem=b.dsem)

    def allgather(self, out, in_):
        if not hasattr(self, "ccsem"):
            self.ccsem = self.S.new_dsem("cc")
        self.S.op("pool", lambda e: e.collective_compute("AllGather", ALU.bypass, replica_groups=[[0, 1, 2, 3], [4, 5, 6, 7]],
                                                         ins=[in_.ap], outs=[out.ap]),
                  reads=in_.bufs, writes=out.bufs, dsem=self.ccsem, inc=1)

    def dbg(self, name, v, shape):
        if name in self.dbg_names:
            o = self.dout("dbg_" + name, shape)
            self.dma(o, v)

    def build(self):
        T, L, TQ = self.T, self.L, self.TQ
        NW = TQ + 4
        NG = T + 16
        P1N = 256
        NT1 = T // P1N
        NMAC = T // 2048
        xT = self.din("xT", [D, T])
        xown = self.din("xown", [D, NW])
        xs = self.din("xs", [D, 16])
        sh = self.din("sh", [L, D, 16])
        wkv = self.din("wkv", [L, 16, 128, 64])
        ck = self.din("ck", [L, 16, KVL, 128])
        cv = self.din("cv", [L, 16, KVL, 128])
        win = self.din("win", [L, 128, 8, NCOL])
        mu_d = self.din("mu", [L, 128, 6])
        pv_d = self.din("pvec", [L, 128, 48])
        lup_d = self.din("lup", [L, 128, 4, 128])
        wout = self.din("wout", [L, 8, 128, 8, 128])
        wup = self.din("wup", [L, 32, 128, 8, 128])
        wdn = self.din("wdn", [L, 8, 128, 4, 1024])
        cst = self.din("cst", [128, 1280])
        ropep = self.din("ropep", [2, 128, T])
        ropes = self.din("ropes", [2, 128, 16])
        sel_d = self.din("sel", [128, 4])
        o_y = self.dout("o_y", [D, NW])
        o_shp = self.dout("o_shp", [L, 128, 8])
        o_shs = self.dout("o_shs", [L, 128, 8, 16])
        o_wkvp = self.dout("o_wkvp", [L, 128, 64])
        o_wkvs = self.dout("o_wkvs", [L, 16, 128, 64])
        o_kp = self.dout("o_kp", [L, 128, KVL])
        o_vp = self.dout("o_vp", [L, 128, KVL])
        o_ks = self.dout("o_ks", [L, 128, 16])
        o_vs = self.dout("o_vs", [L, 128, 16])
        C1 = 1024
        C2 = 256
        NC1 = T // C1
        NC2 = TQ // C2
        ag1_in = [self.dscr("ag1_in%d" % i, [256, C1]) for i in range(NC1)]
        ag1_out = [self.dscr("ag1_out%d" % i, [4 * 256, C1]) for i in range(NC1)]
        ag1s_in = self.dscr("ag1s_in", [256, 16])
        ag1s_out = self.dscr("ag1s_out", [4 * 256, 16])
        ag2_in = [self.dscr("ag2_in%d" % i, [D, C2]) for i in range(NC2)]
        ag2_out = [self.dscr("ag2_out%d" % i, [4 * D, C2]) for i in range(NC2)]
        ag2s_in = self.dscr("ag2s_in", [D, 4])
        ag2s_out = self.dscr("ag2s_out", [4 * D, 4])
        vfirst = self.dscr("vfirst", [128, NG])

        cs = self.sbt("cst", [128, 1280])
        self.ident = cs[:, 0:128]
        perm = cs[:, 128:256]
        bones = cs[:, 256:384]
        amask = cs[:, 384:640]
        MsT = cs[:, 640:768]
        MiT = cs[:, 768:896]
        Ms = cs[:, 896:1024]
        ones = cs[:, 1024:1152]
        sel = self.sbt("sel", [128, 4])
        cvec = self.sbt("cvec", [128, 8])
        mu = self.sbt("mu", [128, 6])
        pv = self.sbt("pv", [128, 48])
        om_ksa = self.sbt("omksa", [128, 1])
        lup = self.sbt("lup", [128, 4, 128])
        Hst = self.sbt("Hst", [128, 64])
        shp_t = self.sbt("shp_t", [128, 8])
        PS = [self.pst("ps%d" % i, [128, 512]) for i in range(8)]
        AW = 44800
        arena_t = self.es.enter_context(self.nc.sbuf_tensor("arena", [128, AW], F32))
        st = {"off": 0}

        def areset():
            st["off"] = 0

        def aalloc(name, n):
            o = st["off"]
            assert o + n <= AW, ("arena overflow", name, o + n)
            st["off"] = o + n
            return Vw(arena_t[:, o:o + n], [Buf(name)])

        self.dma(cs, cst)
        self.dma(sel, sel_d)
        self.memset(cvec[:, 0:1], LN_EPS)
        self.memset(cvec[:, 1:2], GN_EPS)
        self.memset(cvec[:, 2:3], RMS_EPS)
        self.memset(cvec[:, 3:4], 0.0)

        psrr = {"i": 0}

        mmpool = {"p": [0, 1, 2, 3, 4, 5]}

        def nps():
            psrr["i"] = (psrr["i"] + 1) % len(mmpool["p"])
            return PS[mmpool["p"][psrr["i"]]]

        ptrr = {"i": 0}

        def npt():
            ptrr["i"] = (ptrr["i"] + 1) % 2
            return PS[6 + ptrr["i"]]

        PV_DB, PV_IB, PV_VB, PV_KSK, PV_KSA, PV_BON, PV_GNG, PV_GNB = range(8)
        PV_AG = 8
        PV_L1G, PV_L1B, PV_L2G, PV_L2B = 12, 20, 28, 36

        def rwkv_tile(AR, NN, zr, zk, vv, ld, aa, zg_sig_g, YT, state_in, state_out):
            nch = NN // 64
            kk = AR("kk", NN)
            tmp = AR("rtmp", NN)
            tmp2 = AR("rtmp2", NN)
            kp = AR("kp", NN)
            bb = AR("bb", NN)
            Lc = AR("Lc", NN)
            gam = AR("gam", NN)
            Rt = AR("Rt", NN)
            At = AR("At", NN)
            Bt = AR("Bt", NN)
            Kt = AR("Kt", NN)
            Bp = AR("Bp", NN)
            Kp = AR("Kp", NN)
            self.ts(kk, zk, pv[:, PV_KSK:PV_KSK + 1], ALU.mult)
            self.act(tmp, kk, AF.Square)
            for c0 in range(0, NN, 512):
                c1 = min(NN, c0 + 512)
                ps = nps()
                self.mm(ps[:, 0:c1 - c0], bones, tmp[:, c0:c1])
                self.act(tmp2[:, c0:c1], ps[:, 0:c1 - c0], AF.Sqrt)
            self.ts(tmp2, tmp2, 1e-12, ALU.max)
            self.recip(tmp2, tmp2)
            self.tt(kk, kk, tmp2, ALU.mult)
            self.ts(tmp, aa, pv[:, PV_KSA:PV_KSA + 1], ALU.mult, om_ksa[:, 0:1], ALU.add)
            self.tt(kp, zk, tmp, ALU.mult, eng="pool")
            self.tt(bb, kk, aa, ALU.mult, eng="pool")
            for c in range(nch):
                sl = slice(c * 64, (c + 1) * 64)
                self.scan(Lc[:, sl], ones[:, 0:64], ld[:, sl], 0.0, ALU.mult, ALU.add)
            self.act(gam, Lc, AF.Exp)
            self.tt(Rt, zr, gam, ALU.mult)
            self.tt(tmp, Lc, ld, ALU.subtract)
            self.act(tmp, tmp, AF.Exp)
            self.stt(At, kk, -1.0, tmp, ALU.mult, ALU.mult)
            self.act(tmp2, Lc, AF.Exp, scale=-1.0)
            self.tt(Bt, bb, tmp2, ALU.mult)
            self.tt(Kt, kp, tmp2, ALU.mult, eng="pool")
            for c in range(nch):
                sl = slice(c * 64, (c + 1) * 64)
                gC = gam[:, c * 64 + 63:c * 64 + 64]
                self.ts(Bp[:, sl], Bt[:, sl], gC, ALU.mult)
                self.ts(Kp[:, sl], Kt[:, sl], gC, ALU.mult, eng="pool")
            self.ck("rk_pre")
            def mkset(sfx):
                d = {}
                for nm in ("A", "B", "K", "R", "V", "Bp", "Kp"):
                    d["bd" + nm] = AR("bd_" + nm + sfx, 128)
                    self.memset(d["bd" + nm], 0.0, eng="pool")
                d["Pm"] = [AR("Pm%d%s" % (i, sfx), 128) for i in range(2)]
                d["Qm"] = [AR("Qm%d%s" % (i, sfx), 128) for i in range(2)]
                d["Zm"] = [AR("Zm%d%s" % (i, sfx), 128) for i in range(2)]
                for nm, n_ in (("AakT", 128), ("ArbT", 64), ("ArkT", 64), ("Vtb", 128), ("Vts", 64), ("Bptb", 128), ("Kptb", 128), ("mt", 128)):
                    d[nm] = AR(nm + sfx, n_)
                d["cur"] = 0
                return d

            sets = [mkset(""), mkset("_b")]
            bdU = AR("bd_U", 128)
            bdH = AR("bd_H", 128)
            self.memset(bdU, 0.0, eng="pool")
            self.memset(bdH, 0.0, eng="pool")
            RHS = AR("RHS", 64)
            Us = AR("Us", 64)
            no_inv = state_in is not None

            def to_bd(dst, src, e0="dve", e1="pool"):
                self.cp(dst[0:64, 0:64], src[0:64, :], eng=e0)
                self.cp(dst[64:128, 64:128], src[64:128, :], eng=e1)

            def prefix(c, d):
                sl = slice(c * 64, (c + 1) * 64)
                to_bd(d["bdA"], At[:, sl])
                to_bd(d["bdB"], Bt[:, sl], "act", "pool")
                to_bd(d["bdK"], Kt[:, sl])
                to_bd(d["bdR"], Rt[:, sl], "act", "pool")
                to_bd(d["bdV"], vv[:, sl])
                to_bd(d["bdBp"], Bp[:, sl], "act", "pool")
                to_bd(d["bdKp"], Kp[:, sl])
                p1 = nps()
                self.mm(p1[:, 0:128], d["bdA"], d["bdB"])
                self.mm(p1[:, 128:256], d["bdB"], d["bdA"])
                self.mm(p1[:, 256:384], d["bdK"], d["bdA"])
                p2 = nps()
                self.mm(p2[:, 0:128], d["bdB"], d["bdR"])
                self.mm(p2[:, 128:256], d["bdK"], d["bdR"])
                pV = npt()
                self.tr(pV[:, 0:128], d["bdV"], 128)
                self.tt(d["Pm"][0], p1[:, 0:128], Ms, ALU.mult)
                self.tt(d["Qm"][0], p1[:, 128:256], MsT, ALU.mult)
                self.tt(d["AakT"], p1[:, 256:384], MsT, ALU.mult)
                self.tt(d["mt"], p2[:, 0:128], MiT, ALU.mult)
                self.tt(d["ArbT"], d["mt"][:, 0:64], d["mt"][:, 64:128], ALU.add, eng="pool")
                self.tt(d["mt"], p2[:, 128:256], MiT, ALU.mult)
                self.tt(d["ArkT"], d["mt"][:, 0:64], d["mt"][:, 64:128], ALU.add, eng="pool")
                self.cp(d["Vtb"], pV[:, 0:128], eng="act")
                self.tt(d["Vts"], d["Vtb"][:, 0:64], d["Vtb"][:, 64:128], ALU.add, eng="pool")
                p3 = npt()
                self.tr(p3[:, 0:128], d["bdBp"], 128)
                self.tr(p3[:, 128:256], d["bdKp"], 128)
                self.cp(d["Bptb"], p3[:, 0:128], eng="act")
                self.cp(d["Kptb"], p3[:, 128:256], eng="act")
                self.tt(d["Zm"][0], d["Qm"][0], self.ident, ALU.add)
                d["cur"] = 0

            def dround(d):
                cur = d["cur"]
                nx = 1 - cur
                Pm, Qm, Zm = d["Pm"], d["Qm"], d["Zm"]
                pq = nps()
                self.mm(pq[:, 0:128], Pm[cur], Qm[cur])
                self.mm(pq[:, 128:256], Qm[cur], Pm[cur])
                self.cp(Qm[nx], pq[:, 0:128], eng="act")
                self.cp(Pm[nx], pq[:, 128:256], eng="act")
                pz = nps()
                self.mm(pz[:, 0:128], Pm[nx], Zm[cur])
                self.tt(Zm[nx], pz[:, 0:128], Zm[cur], ALU.add)
                d["cur"] = nx

            def suffix(c, d):
                sl = slice(c * 64, (c + 1) * 64)
                if state_in is not None:
                    state_in(c)
                to_bd(bdH, Hst)
                Z = d["Zm"][d["cur"]]
                pr = nps()
                self.mm(pr[:, 0:64], d["bdA"], Hst, start=True, stop=False)
                self.mm(pr[:, 0:64], d["AakT"], d["Vts"], start=False, stop=True)
                self.cp(RHS, pr[:, 0:64], eng="act")
                pu = nps()
                self.mm(pu[:, 0:64], Z, RHS)
                self.cp(Us, pu[:, 0:64], eng="act")
                to_bd(bdU, Us)
                py = nps()
                self.mm(py[:, 0:64], bdH, Rt[:, sl], start=True, stop=False)
                self.mm(py[:, 0:64], bdU, d["ArbT"], start=False, stop=False)
                self.mm(py[:, 0:64], d["Vtb"], d["ArkT"], start=False, stop=True)
                self.cp(YT[:, sl], py[:, 0:64], eng="act")
                ph = nps()
                self.mm(ph[:, 0:64], d["Bptb"], Us, start=True, stop=False)
                self.mm(ph[:, 0:64], d["Kptb"], d["Vts"], start=False, stop=True)
                self.stt(Hst, Hst, gam[:, c * 64 + 63:c * 64 + 64], ph[:, 0:64], ALU.mult, ALU.add)
                if state_out is not None:
                    state_out(c)

            for c0 in range(0, nch, 2):
                if _os.environ.get("SKIP", "") == "rwkv":
                    break
                cs_ = [c for c in (c0, c0 + 1) if c < nch]
                for i_, c in enumerate(cs_):
                    prefix(c, sets[i_])
                if not no_inv:
                    for rnd in range(5):
                        for i_, c in enumerate(cs_):
                            dround(sets[i_])
                for i_, c in enumerate(cs_):
                    suffix(c, sets[i_])
            self.ck("rk_chunks")
            for c0 in range(0, NN, 512):
                c1 = min(NN, c0 + 512)
                w = c1 - c0
                s_ = slice(c0, c1)
                pm = nps()
                self.mm(pm[:, 0:w], bones, YT[:, s_])
                self.act(tmp[:, s_], YT[:, s_], AF.Square)
                pq2 = nps()
                self.mm(pq2[:, 0:w], bones, tmp[:, s_])
                self.ts(tmp2[:, s_], pm[:, 0:w], 1.0 / 64, ALU.mult)
                self.tt(YT[:, s_], YT[:, s_], tmp2[:, s_], ALU.subtract)
                self.tt(tmp2[:, s_], tmp2[:, s_], tmp2[:, s_], ALU.mult)
                self.stt(tmp[:, s_], pq2[:, 0:w], 1.0 / 64, tmp2[:, s_], ALU.mult, ALU.subtract)
                self.act(tmp[:, s_], tmp[:, s_], AF.Sqrt, bias=cvec[:, 1:2])
                self.recip(tmp[:, s_], tmp[:, s_])
                self.tt(YT[:, s_], YT[:, s_], tmp[:, s_], ALU.mult)
                self.ts(YT[:, s_], YT[:, s_], pv[:, PV_GNG:PV_GNG + 1], ALU.mult, pv[:, PV_GNB:PV_GNB + 1], ALU.add)
                self.stt(tmp[:, s_], zr[:, s_], pv[:, PV_BON:PV_BON + 1], kp[:, s_], ALU.mult, ALU.mult)
                pb = nps()
                self.mm(pb[:, 0:w], bones, tmp[:, s_])
                self.tt(tmp[:, s_], pb[:, 0:w], vv[:, s_], ALU.mult)
                self.tt(YT[:, s_], YT[:, s_], tmp[:, s_], ALU.add)
                self.tt(YT[:, s_], YT[:, s_], zg_sig_g[:, s_], ALU.mult)

        def rwkv_prep(AR, NN, l, z, vf_cols, first_layer_store):
            tw = AR("tw", NN)
            ldt = AR("ld", NN)
            at = AR("a", NN)
            gt = AR("g", NN)
            self.act(tw, z["wa"], AF.Tanh)
            self.act(gt, z["g"], AF.Sigmoid)
            for c0 in range(0, NN, 512):
                c1 = min(NN, c0 + 512)
                w = c1 - c0
                s_ = slice(c0, c1)
                p = nps()
                self.mm(p[:, 0:w], lup[:, 0, :], tw[:, s_])
                self.act(ldt[:, s_], p[:, 0:w], AF.Sigmoid, bias=pv[:, PV_DB:PV_DB + 1])
                p = nps()
                self.mm(p[:, 0:w], lup[:, 1, :], z["wa"][:, s_])
                self.act(at[:, s_], p[:, 0:w], AF.Sigmoid, bias=pv[:, PV_IB:PV_IB + 1])
                p = nps()
                self.mm(p[:, 0:w], lup[:, 2, :], gt[:, s_])
                self.cp(gt[:, s_], p[:, 0:w], eng="act")
            self.ts(ldt, ldt, -WDEC, ALU.mult)
            vt = z["v"]
            if l == 0:
                self.dma(vfirst[:, vf_cols], vt)
            else:
                vf = AR("vf", NN)
                vm = AR("vm", NN)
                self.dma(vf, vfirst[:, vf_cols])
                for c0 in range(0, NN, 512):
                    c1 = min(NN, c0 + 512)
                    w = c1 - c0
                    s_ = slice(c0, c1)
                    p = nps()
                    self.mm(p[:, 0:w], lup[0:32, 3, :], z["vr"][0:32, s_])
                    self.act(vm[:, s_], p[:, 0:w], AF.Sigmoid, bias=pv[:, PV_VB:PV_VB + 1])
                self.tt(vf, vf, vt, ALU.subtract)
                self.tt(vf, vf, vm, ALU.mult)
                self.tt(vt, vt, vf, ALU.add)
            return vt, ldt, at, gt

        for l in range(L):
            self.S.barrier()
            areset()

            arc = {}

            def AR(name, n, arc=arc):
                k_ = (name, n)
                if k_ not in arc:
                    arc[k_] = aalloc(name, n)
                return arc[k_]

            Wc = aalloc("Wc", 8 * NCOL).re("p (k n) -> p k n", k=8)
            self.dma(Wc, win[l])
            self.dma(mu, mu_d[l])
            self.dma(pv, pv_d[l])
            self.dma(lup, lup_d[l])
            self.ts(om_ksa, pv[:, PV_KSA:PV_KSA + 1], -1.0, ALU.mult, 1.0, ALU.add)
            self.memset(Hst, 0.0)
            xt = aalloc("xt", 8 * P1N).re("p (k n) -> p k n", k=8)
            xts = aalloc("xts", 8 * 32).re("p (k n) -> p k n", k=8)
            pr6 = aalloc("pr6", 6 * (P1N + 1)).re("p (c n) -> p c n", c=6)
            z6 = aalloc("z6", 6 * P1N).re("p (c n) -> p c n", c=6)
            pq_t = aalloc("pq", P1N)
            pk_t = aalloc("pk", P1N)
            cs_t = aalloc("cossin", 2 * P1N).re("p (c n) -> p c n", c=2)
            KT = [aalloc("KT%d" % i, 2048) for i in range(2)]
            VT = [aalloc("VT%d" % i, 2048) for i in range(2)]
            QT = aalloc("QT", 2048)
            YT = aalloc("YT", P1N)
            acc = aalloc("acc", 2048)
            rb = aalloc("rb", 2048)
            ET = [aalloc("ET%d" % i, 256) for i in range(2)]
            Vg = [aalloc("Vg%d" % i, 130).re("p (h d) -> p h d", h=2) for i in range(2)]
            for i in range(2):
                self.memset(Vg[i], 1.0)
            mark = st["off"]
            self.memset(pr6[:, :, 0:1], 0.0)
            self.ck("init")

            for n in range(NT1):
                if l == 0:
                    src = xT.re("(k p) t -> p k t", p=128)[:, :, n * P1N:(n + 1) * P1N]
                else:
                    r_ = (n * P1N) // TQ
                    loc = (n * P1N) % TQ
                    src = ag2_out[loc // C2].re("(r k p) t -> r p k t", r=4, p=128)[r_]
                self.dma(xt, src)
                if n == NT1 - 1:
                    self.cp(shp_t, xt[:, :, P1N - 1])
                    self.dma(o_shp[l], shp_t)
                self.dma(cs_t, ropep.re("c p t -> p c t")[:, :, n * P1N:(n + 1) * P1N])
                mac = (n * P1N) // 2048
                par = mac % 2
                mcol = (n * P1N) % 2048
                for m in range(9):
                    M = 128 if m < 8 else 32
                    ps = nps()
                    for k in range(8):
                        self.mm(ps[0:M, 0:P1N], Wc[:, k, m * 128:m * 128 + M], xt[:, k, :], start=(k == 0), stop=(k == 7), fast=True)
                    if m == 0:
                        self.cp(pq_t, ps[:, 0:P1N], eng="act")
                    elif m == 1:
                        self.cp(pk_t, ps[:, 0:P1N], eng="act")
                    elif m == 2:
                        self.cp(VT[par][:, mcol:mcol + P1N], ps[:, 0:P1N], eng="act")
                    else:
                        self.cp(pr6[0:M, m - 3, 1:P1N + 1], ps[0:M, 0:P1N], eng="act")
                self.ck("proj")
                for (src_t, dst) in ((pq_t, QT[:, mcol:mcol + P1N]), (pk_t, KT[par][:, mcol:mcol + P1N])):
                    ps = nps()
                    self.mm(ps[:, 0:P1N], perm, src_t)
                    self.tt(dst, ps[:, 0:P1N], cs_t[:, 1, :], ALU.mult)
                    self.tt(src_t, src_t, cs_t[:, 0, :], ALU.mult, eng="pool")
                    self.tt(dst, dst, src_t, ALU.add)
                if n * P1N >= T - KVL:
                    oc = n * P1N - (T - KVL)
                    self.dma(o_kp[l][:, oc:oc + P1N], KT[par][:, mcol:mcol + P1N])
                    self.dma(o_vp[l][:, oc:oc + P1N], VT[par][:, mcol:mcol + P1N])
                self.ck("rope")
                for c in range(6):
                    M = 128 if c < 5 else 32
                    self.tt(z6[0:M, c, :], pr6[0:M, c, 0:P1N], pr6[0:M, c, 1:P1N + 1], ALU.subtract)
                    self.stt(z6[0:M, c, :], z6[0:M, c, :], mu[0:M, c:c + 1], pr6[0:M, c, 1:P1N + 1], ALU.mult, ALU.add)
                    self.cp(pr6[0:M, c, 0:1], pr6[0:M, c, P1N:P1N + 1], eng="pool")
                self.ck("shift")
                z = {"r": z6[:, 0, :], "k": z6[:, 1, :], "v": z6[:, 2, :], "wa": z6[:, 3, :], "g": z6[:, 4, :], "vr": z6[:, 5, :]}
                vt, ldt, at, gt = rwkv_prep(AR, P1N, l, z, slice(n * P1N, (n + 1) * P1N), None)
                self.ck("prep")
                rwkv_tile(AR, P1N, z["r"], z["k"], vt, ldt, at, gt, YT, None, None)
                self.ck("rwkv1")
                self.dma(ag1_in[(n * P1N) // C1][128:256, (n * P1N) % C1:(n * P1N) % C1 + P1N], YT)
                if (n + 1) * P1N % 2048 == 0 and _os.environ.get("SKIP", "") != "att":
                    prev_par = 1 - par
                    mmpool["p"] = [4, 5]
                    for h in range(2):
                        hs = slice(64 * h, 64 * h + 64)
                        first_pat = True
                        for (dil, nblk) in ((1, 16), (4, 4), (16, 1)):
                            pacc = PS[0:4]
                            vgi = 0
                            for r in range(dil):
                                for blk in range(nblk):
                                    u = r * nblk + blk
                                    t0 = r + dil * 128 * blk
                                    cols = slice(t0, t0 + dil * 127 + 1, dil)
                                    have_prev = not (mac == 0 and blk == 0)
                                    if blk > 0:
                                        pt0 = r + dil * 128 * (blk - 1)
                                        pcols = slice(pt0, pt0 + dil * 127 + 1, dil)
                                        kprev, vprev = KT[par], VT[par]
                                    else:
                                        pt0 = r + dil * 128 * (nblk - 1)
                                        pcols = slice(pt0, pt0 + dil * 127 + 1, dil)
                                        kprev, vprev = KT[prev_par], VT[prev_par]
                                    pss = nps()
                                    et = ET[u % 2]
                                    if have_prev:
                                        self.mm(pss[:, 0:128], kprev[hs, pcols], QT[hs, cols])
                                    self.mm(pss[:, 128:256], KT[par][hs, cols], QT[hs, cols])
                                    lo = 0 if have_prev else 128
                                    self.act(et[:, lo:256], pss[:, lo:256], AF.Exp, scale=0.125)
                                    self.tt(et[:, lo:256], et[:, lo:256], amask[:, lo:256], ALU.mult, eng="pool")
                                    pvt = npt()
                                    if have_prev:
                                        self.tr(pvt[:, 0:128], vprev[:, pcols], 128)
                                    self.tr(pvt[:, 128:256], VT[par][:, cols], 128)
                                    vga, vgb = Vg[0], Vg[1]
                                    if have_prev:
                                        self.cp(vga[:, :, 0:64], pvt[:, 0:128].re("p (h d) -> p h d", h=2), eng="dve")
                                    self.cp(vgb[:, :, 0:64], pvt[:, 128:256].re("p (h d) -> p h d", h=2), eng="dve")
                                    pa = pacc[u // 4]
                                    oc = (u % 4) * 128
                                    if have_prev:
                                        self.mm(pa[0:65, oc:oc + 128], vga[:, h, :], et[:, 0:128], start=True, stop=False)
                                    self.mm(pa[0:65, oc:oc + 128], vgb[:, h, :], et[:, 128:256], start=(not have_prev), stop=True)
                            for g4 in range(4):
                                pa = pacc[g4]
                                if dil == 1:
                                    dst = acc[0:65, g4 * 512:(g4 + 1) * 512]
                                    srcv = pa[0:65, :]
                                elif dil == 4:
                                    dst = acc[0:65, :].re("p (b i r) -> p r b i", b=4, i=128, r=4)[:, g4]
                                    srcv = pa[0:65, :].re("p (b i) -> p b i", b=4)
                                else:
                                    dst = acc[0:65, :].re("p (i r) -> p r i", r=16)[:, 4 * g4:4 * g4 + 4]
                                    srcv = pa[0:65, :].re("p (r i) -> p r i", r=4)
                                if first_pat:
                                    self.cp(dst, srcv, eng="dve")
                                else:
                                    self.tt(dst, dst, srcv, ALU.add)
                            first_pat = False
                        for g4 in range(4):
                            pb_ = nps()
                            self.mm(pb_[0:64, :], ones[64:65, 0:64], acc[64:65, g4 * 512:(g4 + 1) * 512])
                            self.recip(rb[0:64, g4 * 512:(g4 + 1) * 512], pb_[0:64, :])
                        self.tt(rb[0:64, :], rb[0:64, :], acc[0:64, :], ALU.mult)
                        for cc_ in range(2048 // C1):
                            self.dma(ag1_in[mac * (2048 // C1) + cc_][64 * h:64 * h + 64, :], rb[0:64, cc_ * C1:(cc_ + 1) * C1])
                    mmpool["p"] = [0, 1, 2, 3, 4, 5]
                    if _os.environ.get("SKIP", "") != "att":
                        for cc_ in range(2048 // C1):
                            self.allgather(ag1_out[mac * (2048 // C1) + cc_], ag1_in[mac * (2048 // C1) + cc_])
            self.ck("prompt_mix")
            if l == 0:
                self.dma(xts[:, :, 0:16], xs.re("(k p) s -> p k s", p=128))
            else:
                for r_ in range(4):
                    self.dma(xts[:, :, 4 * r_:4 * r_ + 4], ag2s_out.re("(r k p) t -> r p k t", r=4, p=128)[r_])
            self.dma(xts[:, :, 16:32], sh[l].re("(k p) s -> p k s", p=128))
            self.dma(o_shs[l], xts[:, :, 0:16])
            SN = 16
            ps6 = AR("ps6", 6 * 32).re("p (c n) -> p c n", c=6)
            zs6 = AR("zs6", 6 * SN).re("p (c n) -> p c n", c=6)
            qs = AR("qs", SN)
            ks = AR("ks", SN)
            vs = AR("vs", SN)
            qk_t = AR("qkt", 2 * SN).re("p (c n) -> p c n", c=2)
            css = AR("css", 2 * SN).re("p (c n) -> p c n", c=2)
            self.dma(css, ropes.re("c p s -> p c s"))
            for m in range(9):
                M = 128 if m < 8 else 32
                ps = nps()
                for k in range(8):
                    self.mm(ps[0:M, 0:32], Wc[:, k, m * 128:m * 128 + M], xts[:, k, :], start=(k == 0), stop=(k == 7))
                if m < 2:
                    self.cp(qk_t[:, m, :], ps[:, 0:SN], eng="act")
                elif m == 2:
                    self.cp(vs, ps[:, 0:SN], eng="act")
                else:
                    self.cp(ps6[0:M, m - 3, :], ps[0:M, 0:32], eng="act")
            for (i_, dst) in ((0, qs), (1, ks)):
                ps = nps()
                self.mm(ps[:, 0:SN], perm, qk_t[:, i_, :])
                self.tt(dst, ps[:, 0:SN], css[:, 1, :], ALU.mult)
                self.tt(qk_t[:, i_, :], qk_t[:, i_, :], css[:, 0, :], ALU.mult)
                self.tt(dst, dst, qk_t[:, i_, :], ALU.add)
            self.dma(o_ks[l], ks)
            self.dma(o_vs[l], vs)
            for c in range(6):
                M = 128 if c < 5 else 32
                self.tt(zs6[0:M, c, :], ps6[0:M, c, 16:32], ps6[0:M, c, 0:16], ALU.subtract)
                self.stt(zs6[0:M, c, :], zs6[0:M, c, :], mu[0:M, c:c + 1], ps6[0:M, c, 0:16], ALU.mult, ALU.add)
            self.ck("sproj")
            accs = AR("accs", 2 * SN).re("p (h n) -> p h n", h=2)
            Kc = [AR("Kc%d" % i, 128) for i in range(2)]
            KcT = [AR("KcT%d" % i, 128) for i in range(2)]
            Vc = [AR("Vc%d" % i, 130).re("p (h d) -> p h d", h=2) for i in range(2)]
            es_t = [AR("es%d" % i, 2) for i in range(2)]
            for i in range(2):
                self.memset(Vc[i], 1.0)
            pacc_s = [PS[0], PS[1]]
            mmpool["p"] = [2, 3, 4, 5]
            it = 0
            for s in range(SN):
                for pi, dil in enumerate((1, 4, 16)):
                    b2 = it % 2
                    it += 1
                    rows = slice(KVL - 128 * dil, KVL, dil)
                    self.dma(Kc[b2], ck[l, s][rows, :])
                    self.dma(Vc[b2][:, :, 0:64], cv[l, s][rows, :].re("p (h d) -> p h d", h=2), eng="act")
                    pt = npt()
                    self.tr(pt[:, 0:128], Kc[b2], 128)
                    self.cp(KcT[b2], pt[:, 0:128], eng="act")
                    for h in range(2):
                        hs = slice(64 * h, 64 * h + 64)
                        pz = nps()
                        self.mm(pz[:, 0:1], KcT[b2][hs, :], qs[hs, s:s + 1])
                        self.act(es_t[b2][:, h:h + 1], pz[:, 0:1], AF.Exp, scale=0.125)
                    for h in range(2):
                        self.mm(pacc_s[h][0:65, s:s + 1], Vc[b2][:, h, :], es_t[b2][:, h:h + 1], start=(pi == 0), stop=(pi == 2))
            qkp = AR("qkp", SN)
            e3 = AR("e3", SN)
            self.tt(qkp, qs, ks, ALU.mult)
            ps = nps()
            self.mm(ps[:, 0:SN], bones, qkp)
            self.act(e3, ps[:, 0:SN], AF.Exp, scale=0.125)
            self.ts(e3, e3, 3.0, ALU.mult)
            atts = AR("atts", SN)
            dens = AR("dens", SN)
            for h in range(2):
                hs = slice(64 * h, 64 * h + 64)
                self.cp(accs[0:65, h, :], pacc_s[h][0:65, 0:SN])
                pb_ = nps()
                self.mm(pb_[0:64, 0:SN], ones[64:65, 0:64], accs[64:65, h, :])
                self.cp(dens[hs, :], pb_[0:64, 0:SN])
                self.tt(dens[hs, :], dens[hs, :], e3[hs, :], ALU.add)
                self.cp(atts[hs, :], accs[0:64, h, :])
                self.tt(qkp[hs, :], e3[hs, :], vs[hs, :], ALU.mult)
                self.tt(atts[hs, :], atts[hs, :], qkp[hs, :], ALU.add)
            mmpool["p"] = [0, 1, 2, 3, 4, 5]
            self.recip(dens, dens)
            self.tt(atts, atts, dens, ALU.mult)
            self.dma(ag1s_in[0:128, :], atts)
            self.ck("satt")
            zsd = {"r": zs6[:, 0, :], "k": zs6[:, 1, :], "v": zs6[:, 2, :], "wa": zs6[:, 3, :], "g": zs6[:, 4, :], "vr": zs6[:, 5, :]}
            vt, ldt, at, gt = rwkv_prep(AR, SN, l, zsd, slice(T, T + 16), None)
            EN = 256
            ex = {}
            for nm in ("r", "k", "v", "ld", "a", "g"):
                ex[nm] = AR("ex_" + nm, EN)
                self.memset(ex[nm], 0.0, eng="pool")
            YS = AR("YS", EN)
            Sld = AR("Sld", 64)
            Sbd = AR("Sbd", 128)
            Sout = AR("Sout", 128)
            rws = AR("rws", SN)
            self.memset(Sbd, 0.0)

            def s_out_to(dst):
                self.cp(Sbd[0:64, 0:64], Hst[0:64, :])
                self.cp(Sbd[64:128, 64:128], Hst[64:128, :], eng="pool")
                p = npt()
                self.tr(p[:, 0:128], Sbd, 128)
                self.cp(Sout, p[:, 0:128], eng="act")
                self.dma(dst[0:64, :], Sout[0:64, 0:64])
                self.dma(dst[64:128, :], Sout[64:128, 64:128])

            s_out_to(o_wkvp[l])
            srcs = {"r": zsd["r"], "k": zsd["k"], "v": vt, "ld": ldt, "a": at, "g": gt}
            for grp in range(4):
                for nm in ("r", "k", "v", "ld", "a", "g"):
                    self.cp(ex[nm][:, 0:EN:64], srcs[nm][:, 4 * grp:4 * grp + 4], eng=("dve" if nm in ("r", "v", "a") else "pool"))

                def s_in(c, l=l, grp=grp):
                    self.dma(Sld, wkv[l, 4 * grp + c])
                    self.cp(Sbd[0:64, 0:64], Sld[0:64, :])
                    self.cp(Sbd[64:128, 64:128], Sld[64:128, :], eng="pool")
                    p = npt()
                    self.tr(p[:, 0:128], Sbd, 128)
                    self.cp(Sout, p[:, 0:128], eng="act")
                    self.tt(Hst, Sout[:, 0:64], Sout[:, 64:128], ALU.add)

                def s_out(c, l=l, grp=grp):
                    s_out_to(o_wkvs[l, 4 * grp + c])

                rwkv_tile(AR, EN, ex["r"], ex["k"], ex["v"], ex["ld"], ex["a"], ex["g"], YS, s_in, s_out)
                self.cp(rws[:, 4 * grp:4 * grp + 4], YS[:, 0:EN:64])
            self.dma(ag1s_in[128:256, :], rws)

            self.ck("srwkv")
            self.allgather(ag1s_out, ag1s_in)
            self.ck("ag1")
            self.S.barrier()
            areset()
            ST = min(1024, TQ)
            NST = TQ // ST
            STW = ST + 4
            A_ = aalloc("A", 8 * STW).re("p (k n) -> p k n", k=8)
            B_ = aalloc("B", 8 * STW).re("p (k n) -> p k n", k=8)
            hT = [aalloc("hT0", 4 * STW).re("p (k n) -> p k n", k=4)] * 2
            stg = aalloc("stg", 8 * 512).re("p (k n) -> p k n", k=8)
            sq = aalloc("sq", 512)
            rs = aalloc("rs", STW)
            mean = aalloc("mean", 512)
            rstd = aalloc("rstd", 512)
            wo_b = [aalloc("wo0", 8 * 128).re("p (k n) -> p k n", k=8)] * 2
            wu_b = [aalloc("wu%d" % i, 4 * 8 * 128).re("p (f k n) -> p f k n", f=4, k=8) for i in range(2)]
            wd_b = [aalloc("wd%d" % i, 4 * 1024).re("p (f n) -> p f n", f=4) for i in range(2)]
            for sti in range(NST):
                last = (sti == NST - 1)
                W = ST + (4 if last else 0)
                ctiles = [(c0, 512) for c0 in range(0, ST, 512)]
                if last:
                    ctiles.append((ST, 4))
                for (c0, w) in ctiles:
                    for q in range(4):
                        if c0 < ST:
                            gc = q * TQ + sti * ST + c0
                            gsrc = ag1_out[gc // C1].re("(r c p) t -> p (r c) t", r=4, c=2, p=128)[:, :, gc % C1:gc % C1 + w]
                        else:
                            gsrc = ag1s_out.re("(r c p) t -> p (r c) t", r=4, c=2, p=128)[:, :, 4 * q:4 * q + 4]
                        self.dma(stg[:, :, 0:w], gsrc)
                        if q == 0:
                            self.ts(A_[:, :, c0:c0 + w], stg[:, :, 0:w], sel[:, 0:1], ALU.mult)
                        else:
                            self.stt(A_[:, :, c0:c0 + w], stg[:, :, 0:w], sel[:, q:q + 1], A_[:, :, c0:c0 + w], ALU.mult, ALU.add)
                oc0 = sti * ST
                if l == 0:
                    self.dma(B_[:, :, 0:ST], xown.re("(k p) t -> p k t", p=128)[:, :, oc0:oc0 + ST])
                    if last:
                        self.dma(B_[:, :, ST:ST + 4], xown.re("(k p) t -> p k t", p=128)[:, :, TQ:TQ + 4])
                else:
                    for i_ in range(ST // C2):
                        self.dma(B_[:, :, i_ * C2:(i_ + 1) * C2], ag2_in[oc0 // C2 + i_].re("(k p) t -> p k t", p=128))
                    if last:
                        self.dma(B_[:, :, ST:ST + 4], ag2s_in.re("(k p) t -> p k t", p=128))
                for (c0, w) in ctiles:
                    pm = nps()
                    for r in range(4):
                        self.act(sq[:, 0:w], A_[:, 2 * r, c0:c0 + w], AF.Square)
                        self.mm(pm[:, 0:w], ones, sq[:, 0:w], start=(r == 0), stop=(r == 3))
                    self.act(rs[:, c0:c0 + w], pm[:, 0:w], AF.Sqrt, bias=cvec[:, 2:3], scale=1.0 / 512)
                self.recip(rs[:, 0:W], rs[:, 0:W])
                for r in range(4):
                    self.stt(A_[:, 2 * r, 0:W], A_[:, 2 * r, 0:W], pv[:, PV_AG + r:PV_AG + r + 1], rs[:, 0:W], ALU.mult, ALU.mult)
                for m in range(8):
                    wb = wo_b[m % 2]
                    self.dma(wb, wout[l, m])
                    for (c0, w) in ctiles:
                        ps = nps()
                        for kc in range(8):
                            src_k = 2 * kc if kc < 4 else 2 * (kc - 4) + 1
                            self.mm(ps[:, 0:w], wb[:, kc, :], A_[:, src_k, c0:c0 + w], start=(kc == 0), stop=(kc == 7), fast=(w >= 256))
                        self.stt(B_[:, m, c0:c0 + w], B_[:, m, c0:c0 + w], ALPHA, ps[:, 0:w], ALU.mult, ALU.add)

                def layer_norm(X, gcol, bcol):
                    for (c0, w) in ctiles:
                        pm = nps()
                        pq2 = nps()
                        for k in range(8):
                            self.mm(pm[:, 0:w], ones, X[:, k, c0:c0 + w], start=(k == 0), stop=(k == 7))
                        for k in range(8):
                            self.act(sq[:, 0:w], X[:, k, c0:c0 + w], AF.Square)
                            self.mm(pq2[:, 0:w], ones, sq[:, 0:w], start=(k == 0), stop=(k == 7))
                        self.ts(mean[:, 0:w], pm[:, 0:w], 1.0 / D, ALU.mult)
                        self.tt(sq[:, 0:w], mean[:, 0:w], mean[:, 0:w], ALU.mult)
                        self.stt(rstd[:, 0:w], pq2[:, 0:w], 1.0 / D, sq[:, 0:w], ALU.mult, ALU.subtract)
                        self.act(rstd[:, 0:w], rstd[:, 0:w], AF.Sqrt, bias=cvec[:, 0:1])
                        self.recip(rstd[:, 0:w], rstd[:, 0:w])
                        for k in range(8):
                            eng = "dve" if k % 2 == 0 else "pool"
                            self.tt(X[:, k, c0:c0 + w], X[:, k, c0:c0 + w], mean[:, 0:w], ALU.subtract, eng=eng)
                            self.tt(X[:, k, c0:c0 + w], X[:, k, c0:c0 + w], rstd[:, 0:w], ALU.mult, eng=eng)
                            self.ts(X[:, k, c0:c0 + w], X[:, k, c0:c0 + w], pv[:, gcol + k:gcol + k + 1], ALU.mult,
                                    pv[:, bcol + k:bcol + k + 1], ALU.add)

                layer_norm(B_, PV_L1G, PV_L1B)
                for k in range(8):
                    self.ts(A_[:, k, 0:W], B_[:, k, 0:W], ALPHA, ALU.mult, eng=("dve" if k % 2 == 0 else "pool"))
                for g in range(8):
                    hb = hT[g % 2]
                    wub = wu_b[g % 2]
                    wdb = wd_b[g % 2]
                    self.dma(wub, wup[l, 4 * g:4 * g + 4].re("f p k n -> p f k n"))
                    self.dma(wdb, wdn[l, g], eng="act")
                    for f in range(4):
                        for (c0, w) in ctiles:
                            ps = nps()
                            for kc in range(8):
                                self.mm(ps[:, 0:w], wub[:, f, kc, :], B_[:, kc, c0:c0 + w], start=(kc == 0), stop=(kc == 7), fast=(w >= 256))
                            self.act(hb[:, f, c0:c0 + w], ps[:, 0:w], AF.Relu)
                            self.tt(hb[:, f, c0:c0 + w], hb[:, f, c0:c0 + w], hb[:, f, c0:c0 + w], ALU.mult, eng="pool")
                    for m in range(8):
                        for (c0, w) in ctiles:
                            ps = nps()
                            for f in range(4):
                                self.mm(ps[:, 0:w], wdb[:, f, m * 128:(m + 1) * 128], hb[:, f, c0:c0 + w], start=(f == 0), stop=(f == 3), fast=(w >= 256))
                            self.tt(A_[:, m, c0:c0 + w], A_[:, m, c0:c0 + w], ps[:, 0:w], ALU.add)
                layer_norm(A_, PV_L2G, PV_L2B)
                if l == L - 1:
                    self.dma(o_y.re("(k p) t -> p k t", p=128)[:, :, oc0:oc0 + ST], A_[:, :, 0:ST])
                    if last:
                        self.dma(o_y.re("(k p) t -> p k t", p=128)[:, :, TQ:TQ + 4], A_[:, :, ST:ST + 4])
                else:
                    for i_ in range(ST // C2):
                        self.dma(ag2_in[oc0 // C2 + i_].re("(k p) t -> p k t", p=128), A_[:, :, i_ * C2:(i_ + 1) * C2])
                    if last:
                        self.dma(ag2s_in.re("(k p) t -> p k t", p=128), A_[:, :, ST:ST + 4])
                    for i_ in range(ST // C2):
                        self.allgather(ag2_out[oc0 // C2 + i_], ag2_in[oc0 // C2 + i_])
                    if last:
                        self.allgather(ag2s_out, ag2s_in)


        self.S.dead = False
        self.S.barrier()
        block = self.es.enter_context(self.nc.Block())
        self.S.emit(block)
        self.es.close()
        return self.nc


def _consts(T):
    ident = np.eye(128, dtype=np.float32)
    perm = np.zeros((128, 128), np.float32)
    for m in range(128):
        h, i = divmod(m, 64)
        perm[h * 64 + (i + 32) % 64, m] = 1.0
    bones = np.kron(np.eye(2, dtype=np.float32), np.ones((64, 64), np.float32))
    kq = np.arange(128)
    prevm = (kq[:, None] >= kq[None, :]).astype(np.float32)
    curm = (kq[:, None] <= kq[None, :]).astype(np.float32)
    amask = np.concatenate([prevm, curm], 1)
    i64 = np.arange(64)
    sT = (i64[:, None] < i64[None, :]).astype(np.float32)
    iT = (i64[:, None] <= i64[None, :]).astype(np.float32)
    sN = (i64[None, :] < i64[:, None]).astype(np.float32)
    e2 = np.eye(2, dtype=np.float32)
    MsT, MiT, Ms = np.kron(e2, sT), np.kron(e2, iT), np.kron(e2, sN)
    ones = np.ones((128, 128), np.float32)
    pad = np.zeros((128, 128), np.float32)
    cst = np.concatenate([ident, perm, bones, amask, MsT, MiT, Ms, ones, pad], 1).astype(np.float32)
    half = 32
    inv = (10000.0 ** (-np.arange(half, dtype=np.float32) * np.float32(2.0 / 64))).astype(np.float32)

    def tab(pos):
        ang = pos.astype(np.float32)[:, None] * inv[None, :]
        c, s = np.cos(ang).astype(np.float32), np.sin(ang).astype(np.float32)
        cos64 = np.concatenate([c, c], 1).T
        sin64 = np.concatenate([-s, s], 1).T
        return np.stack([np.tile(cos64, (2, 1)), np.tile(sin64, (2, 1))]).astype(np.float32)

    ropep = tab(np.arange(T))
    ropes = tab(np.full((16,), PAST))
    return cst, ropep, ropes


_CACHE = {}


def _get_nc(T, L, dbg=()):
    key = (T, L, tuple(dbg))
    if key not in _CACHE:
        _CACHE[key] = KB(T, L, dbg).build()
    return _CACHE[key]


def run(inp, T, L, dbg=()):
    f = lambda a: np.ascontiguousarray(a, dtype=np.float32)
    TQ = T // 4
    cst, ropep, ropes = _consts(T)
    ATT_W = 512
    RW0 = 3 * ATT_W
    in_maps = []
    for c in range(8):
        b, j = divmod(c, 4)
        hsl = slice(128 * j, 128 * j + 128)
        sg = slice(16 * b, 16 * b + 16)
        m = {}
        m["xT"] = f(inp["x_prompt"][b].T)
        own = np.concatenate([inp["x_prompt"][b, j * TQ:(j + 1) * TQ], inp["x_sample"][16 * b + 4 * j:16 * b + 4 * j + 4, 0]], 0)
        m["xown"] = f(own.T)
        m["xs"] = f(inp["x_sample"][sg, 0].T)
        m["sh"] = f(np.transpose(inp["state_shift"][:L, sg], (0, 2, 1)))
        m["wkv"] = f(inp["state_wkv"][:L, sg, 2 * j:2 * j + 2].reshape(L, 16, 128, 64))
        m["ck"] = f(inp["cache_k"][:L, sg, :, 2 * j:2 * j + 2].reshape(L, 16, KVL, 128))
        m["cv"] = f(inp["cache_v"][:L, sg, :, 2 * j:2 * j + 2].reshape(L, 16, KVL, 128))
        win = np.zeros((L, D, NCOL), np.float32)
        mu = np.zeros((L, 128, 6), np.float32)
        pvec = np.zeros((L, 128, 48), np.float32)
        lup = np.zeros((L, 128, 4, 128), np.float32)
        for l in range(L):
            w = inp["w_in"][l]
            cols = []
            for blk in range(3):
                cols.append(w[:, blk * ATT_W + 128 * j: blk * ATT_W + 128 * j + 128])
            for blk in range(3):
                cols.append(w[:, RW0 + blk * 512 + 128 * j: RW0 + blk * 512 + 128 * j + 128])
            o = RW0 + 3 * 512
            cols.append(w[:, o:o + 128])
            cols.append(w[:, o + 128:o + 256])
            win[l, :, :1024] = np.concatenate(cols, 1)
            smu = inp["shift_mu"][l]
            for blk in range(3):
                mu[l, :, blk] = smu[blk * 512 + 128 * j: blk * 512 + 128 * j + 128]
            mu[l, :, 3] = smu[1536:1664]
            mu[l, :, 4] = smu[1664:1792]
            if l > 0:
                win[l, :, 1024:1056] = inp["w_vres_in"][l - 1]
                mu[l, :32, 5] = inp["vres_mu"][l - 1]
                pvec[l, :, 2] = inp["vres_base"][l - 1][hsl]
                lup[l, :32, 3, :] = inp["vres_up"][l - 1][:, hsl]
            pvec[l, :, 0] = inp["decay_base"][l][hsl]
            pvec[l, :, 1] = inp["iclr_base"][l][hsl]
            pvec[l, :, 3] = inp["key_scale_k"][l][hsl]
            pvec[l, :, 4] = inp["key_scale_a"][l][hsl]
            pvec[l, :, 5] = inp["bonus_rk"][l][hsl]
            pvec[l, :, 6] = inp["gn_g"][l][hsl]
            pvec[l, :, 7] = inp["gn_b"][l][hsl]
            pvec[l, :, 8:12] = inp["att_gain"][l].reshape(4, 128).T
            pvec[l, :, 12:20] = inp["ln1_g"][l].reshape(8, 128).T
            pvec[l, :, 20:28] = inp["ln1_b"][l].reshape(8, 128).T
            pvec[l, :, 28:36] = inp["ln2_g"][l].reshape(8, 128).T
            pvec[l, :, 36:44] = inp["ln2_b"][l].reshape(8, 128).T
            lup[l, :64, 0, :] = inp["decay_up"][l][:, hsl]
            lup[l, 64:, 1, :] = inp["iclr_up"][l][:, hsl]
            lup[l, :, 2, :] = inp["gate_up"][l][:, hsl]
        m["win"] = f(win.reshape(L, 8, 128, NCOL).transpose(0, 2, 1, 3))
        m["mu"] = mu
        m["pvec"] = pvec
        m["lup"] = lup
        m["wout"] = f(inp["w_out"][:L].reshape(L, 8, 128, 8, 128).transpose(0, 3, 2, 1, 4))
        m["wup"] = f(inp["w_ff_up"][:L].reshape(L, 8, 128, 32, 128).transpose(0, 3, 2, 1, 4))
        m["wdn"] = f(inp["w_ff_down"][:L].reshape(L, 8, 4, 128, 1024).transpose(0, 1, 3, 2, 4))
        m["cst"] = cst
        m["ropep"] = ropep
        m["ropes"] = ropes
        sel = np.zeros((128, 4), np.float32)
        sel[:, j] = 1.0
        m["sel"] = sel
        in_maps.append(m)
    nc = _get_nc(T, L, dbg)
    res = run_bass_kernel_spmd(nc, in_maps, core_ids=list(range(8)))
    R = res.results
    B = 2
    y_p = np.zeros((B, T, D), np.float32)
    y_s = np.zeros((32, 1, D), np.float32)
    shp = np.zeros((L, B, D), np.float32)
    shs = np.zeros((L, 32, D), np.float32)
    wkvp = np.zeros((L, B, 8, 64, 64), np.float32)
    wkvs = np.zeros((L, 32, 8, 64, 64), np.float32)
    kp = np.zeros((L, B, KVL, 8, 64), np.float32)
    vp = np.zeros((L, B, KVL, 8, 64), np.float32)
    ksn = np.zeros((L, 32, 1, 8, 64), np.float32)
    vsn = np.zeros((L, 32, 1, 8, 64), np.float32)
    for c in range(8):
        b, j = divmod(c, 4)
        r = R[c]
        oy = r["o_y"]
        y_p[b, j * TQ:(j + 1) * TQ] = oy[:, :TQ].T
        y_s[16 * b + 4 * j:16 * b + 4 * j + 4, 0] = oy[:, TQ:].T
        if j == 0:
            shp[:, b] = r["o_shp"].transpose(0, 2, 1).reshape(L, D)
            shs[:, 16 * b:16 * b + 16] = r["o_shs"].transpose(0, 3, 2, 1).reshape(L, 16, D)
        wkvp[:, b, 2 * j:2 * j + 2] = r["o_wkvp"].reshape(L, 2, 64, 64)
        wkvs[:, 16 * b:16 * b + 16, 2 * j:2 * j + 2] = r["o_wkvs"].reshape(L, 16, 2, 64, 64)
        kp[:, b, :, 2 * j:2 * j + 2] = r["o_kp"].reshape(L, 2, 64, KVL).transpose(0, 3, 1, 2)
        vp[:, b, :, 2 * j:2 * j + 2] = r["o_vp"].reshape(L, 2, 64, KVL).transpose(0, 3, 1, 2)
        ksn[:, 16 * b:16 * b + 16, 0, 2 * j:2 * j + 2] = r["o_ks"].reshape(L, 2, 64, 16).transpose(0, 3, 1, 2)
        vsn[:, 16 * b:16 * b + 16, 0, 2 * j:2 * j + 2] = r["o_vs"].reshape(L, 2, 64, 16).transpose(0, 3, 1, 2)
    outs = (y_p, y_s, shp, shs, wkvp, wkvs, kp, vp, ksn, vsn)
    return outs, R


def kernel(**inputs):
    inp = {k: np.asarray(v) for k, v in inputs.items()}
    outs, _ = run(inp, T=8192, L=4)
    return outs
```
